# Optimizing a Trainium2 kernel written in Bass

```python
import jax
import jax.numpy as jnp
from jax import lax
import numpy as np

D_MODEL = 4096
BATCH = 4
SEQ = 4096
DEPTH = 2

CTX_LEN = 256
GRID_W = 64
HEAD_DIM = 128
ATT_Q_HEADS = 16
ATT_KV_HEADS = 4
ATT_GROUP = ATT_Q_HEADS // ATT_KV_HEADS
ATT_WIDTH = ATT_Q_HEADS * HEAD_DIM
ATT_KV_WIDTH = ATT_KV_HEADS * HEAD_DIM
Q_BLOCK = 128
ROPE_THETA = 10000.0
GMLP_WIDTH = D_MODEL // 4
GMLP_CHUNK = 128
GMLP_GROUPS = 8
GMLP_GROUP_DIM = GMLP_WIDTH // GMLP_GROUPS
MLSTM_WIDTH = D_MODEL // 4
MLSTM_HEADS = 4
MLSTM_HEAD_DIM = MLSTM_WIDTH // MLSTM_HEADS
MLSTM_CHUNK = 128
MIX_WIDTH = ATT_WIDTH + GMLP_WIDTH + MLSTM_WIDTH
N_EXPERTS = 16
EXPERT_FF = D_MODEL // 4
CAPACITY_FACTOR = 2
N_MOD = 6
ALPHA = (2 * DEPTH) ** 0.25
BETA = (8 * DEPTH) ** -0.25
EPS = 1e-6

PROJ_LAYOUT = (
    ("att_q", ATT_WIDTH), ("att_k", ATT_KV_WIDTH), ("att_v", ATT_KV_WIDTH),
    ("gm_u", GMLP_WIDTH), ("gm_v", GMLP_WIDTH),
    ("ml_q", MLSTM_WIDTH), ("ml_k", MLSTM_WIDTH), ("ml_v", MLSTM_WIDTH), ("ml_o", MLSTM_WIDTH),
    ("ml_gates", 4 * MLSTM_HEADS),
)
PROJ_WIDTH = sum(width for _, width in PROJ_LAYOUT)
CTX_STATE_PARTS = ("att_k", "att_v", "ml_k", "ml_v", "ml_gates")

kernel_name = "hybrid_dit_attn_gmlp_mlstm_ecmoe"


def layer_norm(t, g=None, b=None):
    t32 = t.astype(jnp.float32)
    mu = jnp.mean(t32, axis=-1, keepdims=True)
    var = jnp.mean(jnp.square(t32 - mu), axis=-1, keepdims=True)
    y = (t32 - mu) * lax.rsqrt(var + EPS)
    if g is not None:
        y = y * g.astype(jnp.float32) + b.astype(jnp.float32)
    return y.astype(t.dtype)


def rms_norm(t, g):
    t32 = t.astype(jnp.float32)
    y = t32 * lax.rsqrt(jnp.mean(t32 * t32, axis=-1, keepdims=True) + EPS) * g.astype(jnp.float32)
    return y.astype(t.dtype)


def modulate(t, shift, scale):
    return layer_norm(t) * (1.0 + scale) + shift


def proj_offsets():
    offs, start = {}, 0
    for name, width in PROJ_LAYOUT:
        offs[name] = (start, width)
        start += width
    return offs


def project(h, w, names=None):
    offs = proj_offsets()
    if names is None:
        p = h @ w
        return {nm: p[..., s:s + wd] for nm, (s, wd) in offs.items()}
    return {nm: h @ w[:, offs[nm][0]:offs[nm][0] + offs[nm][1]] for nm in names}


def split_heads(t, n_heads, head_dim):
    return t.reshape(*t.shape[:-1], n_heads, head_dim)


def axial_rope_tables(n):
    rows = n // GRID_W
    row = jnp.repeat(jnp.arange(rows), GRID_W).astype(jnp.float32)
    col = jnp.tile(jnp.arange(GRID_W), rows).astype(jnp.float32)
    n_freq = HEAD_DIM // 4
    inv_freq = ROPE_THETA ** (-jnp.arange(n_freq, dtype=jnp.float32) / n_freq)
    ang = jnp.stack([row[:, None] * inv_freq, col[:, None] * inv_freq], axis=1)
    return jnp.cos(ang)[:, None], jnp.sin(ang)[:, None]


def apply_rope(t, cos, sin):
    tr = t.astype(jnp.float32).reshape(*t.shape[:-1], 2, 2, HEAD_DIM // 4)
    t1, t2 = tr[..., 0, :], tr[..., 1, :]
    out = jnp.stack([t1 * cos - t2 * sin, t2 * cos + t1 * sin], axis=-2)
    return out.reshape(t.shape).astype(t.dtype)


def gqa_heads(t):
    b_, n = t.shape[:2]
    return t.reshape(b_, n, ATT_KV_HEADS, ATT_GROUP, HEAD_DIM).transpose(0, 2, 3, 1, 4)


def kv_heads(t):
    return t.transpose(0, 2, 1, 3)


def merge_gqa(o):
    b_, _, _, n, _ = o.shape
    return o.transpose(0, 3, 1, 2, 4).reshape(b_, n, ATT_WIDTH)


def attend(q, k, v):
    s = jnp.einsum("bkgqd,bksd->bkgqs", q, k).astype(jnp.float32) * HEAD_DIM ** -0.5
    p = jax.nn.softmax(s, axis=-1).astype(v.dtype)
    return jnp.einsum("bkgqs,bksd->bkgqd", p, v)


def blocked_attention(q, k, v):
    bsz, kvh, grp, n, hd = q.shape
    nb = n // Q_BLOCK
    qb = jnp.moveaxis(q.reshape(bsz, kvh, grp, nb, Q_BLOCK, hd), 3, 0)
    o = lax.map(lambda qblk: attend(qblk, k, v), qb)
    return jnp.moveaxis(o, 0, 3).reshape(bsz, kvh, grp, n, hd)


def spatial_gating(pu, pv, ln_g, ln_b, w_s, b_s):
    bsz, n, _ = pu.shape
    u = jax.nn.gelu(pu)
    v = layer_norm(jax.nn.gelu(pv), ln_g, ln_b)
    nc = n // GMLP_CHUNK
    vb = v.reshape(bsz, nc, GMLP_CHUNK, GMLP_GROUPS, GMLP_GROUP_DIM)
    sv = jnp.einsum("gpq,bcqgd->bcpgd", w_s, vb) + jnp.swapaxes(b_s, 0, 1)[:, :, None]
    return u * sv.reshape(bsz, n, GMLP_WIDTH)


def mlstm_heads(t):
    return split_heads(t, MLSTM_HEADS, MLSTM_HEAD_DIM).transpose(0, 2, 1, 3).astype(jnp.float32)


def mlstm_gates(pg, b_g):
    g = jnp.swapaxes((pg + b_g).astype(jnp.float32), 1, 2)
    i_f, f_f, i_b, f_b = jnp.split(g, 4, axis=1)
    return i_f, jax.nn.log_sigmoid(f_f), i_b, jax.nn.log_sigmoid(f_b)


def mlstm_zero_state(bsz):
    return (jnp.zeros((bsz, MLSTM_HEADS, MLSTM_HEAD_DIM, MLSTM_HEAD_DIM), jnp.float32),
            jnp.zeros((bsz, MLSTM_HEADS, MLSTM_HEAD_DIM), jnp.float32),
            jnp.zeros((bsz, MLSTM_HEADS), jnp.float32))


def mlstm_scan(q, k, v, ig, lf, state, with_output):
    b_, h_, n, dh = k.shape
    nc = n // MLSTM_CHUNK

    def chunks(t):
        return jnp.moveaxis(t.reshape(b_, h_, nc, MLSTM_CHUNK, *t.shape[3:]), 2, 0)

    tril = jnp.tril(jnp.ones((MLSTM_CHUNK, MLSTM_CHUNK), dtype=bool))

    def step(carry, inp):
        c_mat, n_vec, m = carry
        kc, vc, ic, fc = inp[:4]
        bcum = jnp.cumsum(fc, axis=-1)
        b_last = bcum[..., -1]
        g = b_last[..., None] - bcum + ic
        m_new = jnp.maximum(b_last + m, jnp.max(g, axis=-1))
        w_prev = jnp.exp(b_last + m - m_new)
        kw = kc * jnp.exp(g - m_new[..., None])[..., None]
        c_new = w_prev[..., None, None] * c_mat + jnp.einsum("bhsd,bhse->bhde", kw, vc)
        n_new = w_prev[..., None] * n_vec + jnp.sum(kw, axis=2)
        if not with_output:
            return (c_new, n_new, m_new), None
        qc = inp[4]
        a = bcum + m[..., None]
        dlog = jnp.where(tril, bcum[..., :, None] - bcum[..., None, :] + ic[..., None, :], -jnp.inf)
        mt = jnp.maximum(a, jnp.max(dlog, axis=-1))
        s = jnp.einsum("bhtd,bhsd->bhts", qc, kc) * jnp.exp(dlog - mt[..., None])
        wa = jnp.exp(a - mt)
        num = wa[..., None] * jnp.einsum("bhtd,bhde->bhte", qc, c_mat) + jnp.einsum("bhts,bhse->bhte", s, vc)
        den = wa * jnp.einsum("bhtd,bhd->bht", qc, n_vec) + jnp.sum(s, axis=-1)
        hc = num / jnp.maximum(jnp.abs(den), jnp.exp(-mt))[..., None]
        return (c_new, n_new, m_new), hc

    xs = (chunks(k), chunks(v), chunks(ig), chunks(lf))
    if with_output:
        xs = xs + (chunks(q),)
    final, hs = lax.scan(step, state, xs)
    if not with_output:
        return final, None
    return final, jnp.moveaxis(hs, 0, 2).reshape(b_, h_, n, dh)


def bidir_mlstm(q, k, v, i_f, lf_f, i_b, lf_b, st_f, st_b, with_output):
    def flip(t):
        return None if t is None else jnp.flip(t, axis=2)
    fin_f, h_f = mlstm_scan(q, k, v, i_f, lf_f, st_f, with_output)
    fin_b, h_b = mlstm_scan(flip(q), flip(k), flip(v), flip(i_b), flip(lf_b), st_b, with_output)
    h = h_f + flip(h_b) if with_output else None
    return h, fin_f, fin_b


def mlstm_output(h, po, gain):
    b_, _, n, _ = h.shape
    hn = h * lax.rsqrt(jnp.mean(h * h, axis=-1, keepdims=True) + EPS)
    hn = hn.transpose(0, 2, 1, 3).reshape(b_, n, MLSTM_WIDTH) * gain.astype(jnp.float32)
    return (jax.nn.sigmoid(po.astype(jnp.float32)) * hn).astype(po.dtype)


def expert_choice_ffn(h, w_router, w_gate, w_up, w_down):
    bsz, n, _ = h.shape
    cap = CAPACITY_FACTOR * n // N_EXPERTS
    aff = jax.nn.softmax((h @ w_router).astype(jnp.float32), axis=-1)
    gate, idx = lax.top_k(jnp.swapaxes(aff, 1, 2), cap)
    bidx = jnp.arange(bsz)[:, None, None]
    xe = h[bidx, idx]
    hid = jax.nn.silu(jnp.einsum("becd,edf->becf", xe, w_gate)) * jnp.einsum("becd,edf->becf", xe, w_up)
    ye = jnp.einsum("becf,efd->becd", hid, w_down) * gate[..., None].astype(h.dtype)
    return jnp.zeros_like(h).at[bidx, idx].add(ye)


def setup_inputs(seed: int = 0) -> dict:
    key = jax.random.key(seed)
    ks = jax.random.split(key, 24)
    f32 = jnp.float32
    d = D_MODEL

    def nrm(k, shape, scale):
        return jax.random.normal(k, shape, f32) * scale

    gate_bias_centre = jnp.repeat(jnp.array([0.0, 3.0, 0.0, 3.0], f32), MLSTM_HEADS)
    return {
        "x": nrm(ks[0], (BATCH, SEQ, d), 1.0),
        "c": nrm(ks[1], (BATCH, d), 1.0),
        "ctx": nrm(ks[2], (BATCH, CTX_LEN, d), 1.0),
        "c_ctx": nrm(ks[3], (d,), 1.0),
        "w_mod": nrm(ks[4], (DEPTH, d, N_MOD * d), 0.5 * d ** -0.5),
        "b_mod": nrm(ks[5], (DEPTH, N_MOD * d), 0.02),
        "w_in": nrm(ks[6], (DEPTH, d, PROJ_WIDTH), d ** -0.5),
        "q_gain": 1.0 + nrm(ks[7], (DEPTH, HEAD_DIM), 0.02),
        "k_gain": 1.0 + nrm(ks[8], (DEPTH, HEAD_DIM), 0.02),
        "gm_ln_g": 1.0 + nrm(ks[9], (DEPTH, GMLP_WIDTH), 0.02),
        "gm_ln_b": nrm(ks[10], (DEPTH, GMLP_WIDTH), 0.02),
        "w_spatial": nrm(ks[11], (DEPTH, GMLP_GROUPS, GMLP_CHUNK, GMLP_CHUNK), GMLP_CHUNK ** -0.5),
        "b_spatial": 1.0 + nrm(ks[12], (DEPTH, GMLP_GROUPS, GMLP_CHUNK), 0.02),
        "b_gates": gate_bias_centre[None] + nrm(ks[13], (DEPTH, 4 * MLSTM_HEADS), 0.1),
        "ml_gain": 1.0 + nrm(ks[14], (DEPTH, MLSTM_WIDTH), 0.02),
        "w_out": nrm(ks[15], (DEPTH, MIX_WIDTH, d), BETA * MIX_WIDTH ** -0.5),
        "ln1_g": 1.0 + nrm(ks[16], (DEPTH, d), 0.02),
        "ln1_b": nrm(ks[17], (DEPTH, d), 0.02),
        "w_router": nrm(ks[18], (DEPTH, d, N_EXPERTS), d ** -0.5),
        "w_e_gate": nrm(ks[19], (DEPTH, N_EXPERTS, d, EXPERT_FF), d ** -0.5),
        "w_e_up": nrm(ks[20], (DEPTH, N_EXPERTS, d, EXPERT_FF), d ** -0.5),
        "w_e_down": nrm(ks[21], (DEPTH, N_EXPERTS, EXPERT_FF, d), BETA * EXPERT_FF ** -0.5),
        "ln2_g": 1.0 + nrm(ks[22], (DEPTH, d), 0.02),
        "ln2_b": nrm(ks[23], (DEPTH, d), 0.02),
    }


def reference(x, c, ctx, c_ctx, w_mod, b_mod, w_in, q_gain, k_gain, gm_ln_g, gm_ln_b, w_spatial,
              b_spatial, b_gates, ml_gain, w_out, ln1_g, ln1_b, w_router, w_e_gate, w_e_up, w_e_down,
              ln2_g, ln2_b):
    bsz, n, _ = x.shape
    cos, sin = axial_rope_tables(n)
    zero = mlstm_zero_state(bsz)
    k_scale = MLSTM_HEAD_DIM ** -0.5
    h_lat, h_ctx = x, ctx
    for l in range(DEPTH):
        last = l == DEPTH - 1
        mod_l = jnp.split((jax.nn.silu(c) @ w_mod[l] + b_mod[l])[:, None, :], N_MOD, axis=-1)
        mod_c = jnp.split(jax.nn.silu(c_ctx) @ w_mod[l] + b_mod[l], N_MOD, axis=-1)

        a_lat = modulate(h_lat, mod_l[0], mod_l[1])
        a_ctx = modulate(h_ctx, mod_c[0], mod_c[1])
        p = project(a_lat, w_in[l])
        pc = project(a_ctx, w_in[l], CTX_STATE_PARTS if last else None)

        k_c = kv_heads(rms_norm(split_heads(pc["att_k"], ATT_KV_HEADS, HEAD_DIM), k_gain[l]))
        v_c = kv_heads(split_heads(pc["att_v"], ATT_KV_HEADS, HEAD_DIM))
        q_l = gqa_heads(apply_rope(rms_norm(split_heads(p["att_q"], ATT_Q_HEADS, HEAD_DIM), q_gain[l]), cos, sin))
        k_l = kv_heads(apply_rope(rms_norm(split_heads(p["att_k"], ATT_KV_HEADS, HEAD_DIM), k_gain[l]), cos, sin))
        v_l = kv_heads(split_heads(p["att_v"], ATT_KV_HEADS, HEAD_DIM))
        att_l = merge_gqa(blocked_attention(q_l, jnp.concatenate([k_c, k_l], axis=2),
                                            jnp.concatenate([v_c, v_l], axis=2)))

        gm_l = spatial_gating(p["gm_u"], p["gm_v"], gm_ln_g[l], gm_ln_b[l], w_spatial[l], b_spatial[l])

        i_fc, lf_fc, i_bc, lf_bc = mlstm_gates(pc["ml_gates"], b_gates[l])
        h_c, st_f, st_b = bidir_mlstm(None if last else mlstm_heads(pc["ml_q"]),
                                      mlstm_heads(pc["ml_k"]) * k_scale, mlstm_heads(pc["ml_v"]),
                                      i_fc, lf_fc, i_bc, lf_bc, zero, zero, not last)
        i_fl, lf_fl, i_bl, lf_bl = mlstm_gates(p["ml_gates"], b_gates[l])
        h_l, _, _ = bidir_mlstm(mlstm_heads(p["ml_q"]), mlstm_heads(p["ml_k"]) * k_scale,
                                mlstm_heads(p["ml_v"]), i_fl, lf_fl, i_bl, lf_bl, st_f, st_b, True)
        ml_l = mlstm_output(h_l, p["ml_o"], ml_gain[l])

        mix_l = jnp.concatenate([att_l, gm_l, ml_l], axis=-1) @ w_out[l]
        new_lat = layer_norm(ALPHA * h_lat + mod_l[2] * mix_l, ln1_g[l], ln1_b[l])
        if not last:
            q_c = gqa_heads(rms_norm(split_heads(pc["att_q"], ATT_Q_HEADS, HEAD_DIM), q_gain[l]))
            att_c = merge_gqa(attend(q_c, k_c, v_c))
            gm_c = spatial_gating(pc["gm_u"], pc["gm_v"], gm_ln_g[l], gm_ln_b[l], w_spatial[l], b_spatial[l])
            ml_c = mlstm_output(h_c, pc["ml_o"], ml_gain[l])
            mix_c = jnp.concatenate([att_c, gm_c, ml_c], axis=-1) @ w_out[l]
            h_ctx = layer_norm(ALPHA * h_ctx + mod_c[2] * mix_c, ln1_g[l], ln1_b[l])
        h_lat = new_lat

        b_lat = modulate(h_lat, mod_l[3], mod_l[4])
        moe_l = expert_choice_ffn(b_lat, w_router[l], w_e_gate[l], w_e_up[l], w_e_down[l])
        h_lat = layer_norm(ALPHA * h_lat + mod_l[5] * moe_l, ln2_g[l], ln2_b[l])
        if not last:
            b_ctx = modulate(h_ctx, mod_c[3], mod_c[4])
            moe_c = expert_choice_ffn(b_ctx, w_router[l], w_e_gate[l], w_e_up[l], w_e_down[l])
            h_ctx = layer_norm(ALPHA * h_ctx + mod_c[5] * moe_c, ln2_g[l], ln2_b[l])
    return h_lat
```

```python
import numpy as np
import concourse.bass as bass
import concourse.mybir as mybir
from concourse.bass_utils import run_bass_kernel_spmd

F32 = mybir.dt.float32
BF16 = mybir.dt.bfloat16
I32 = mybir.dt.int32
AF = mybir.ActivationFunctionType
ALU = mybir.AluOpType
AX = mybir.AxisListType

NCORES = 8
SAME_ENGINE_SYNC = True


class Tile:
    def __init__(self, name, t, psum=False):
        self.name = name
        self.t = t
        self.psum = psum

    def __getitem__(self, k):
        return self.t[k]


class _Stage:
    def __init__(self, P):
        self.P = P

    def __enter__(self):
        import contextlib
        self.prev = self.P.stack
        self.P.stack = contextlib.ExitStack()
        self.P.stack.__enter__()
        return self

    def __exit__(self, *a):
        self.P.barrier()
        self.P.stack.__exit__(*a)
        self.P.stack = self.prev
        return False


class Prog:
    ENGS = ("tensor", "vector", "scalar", "gpsimd", "sync")

    def __init__(self):
        self.nc = bass.Bass("TRN2", target_bir_lowering=False)
        self.ops = {e: [] for e in self.ENGS}
        self.res = {}
        self.dma_cnt = {}
        self.dma_map = {}
        self.waited = {e: {} for e in self.ENGS}
        self.n_tiles = 0
        self.out_names = []
        self.stack = None
        self.uid = 0
        self.n_barriers = 0

    def inp(self, name, shape, dtype=F32):
        return self.nc.dram_tensor(name, list(shape), dtype, kind="ExternalInput").ap()

    def out(self, name, shape, dtype=F32):
        self.out_names.append(name)
        return self.nc.dram_tensor(name, list(shape), dtype, kind="ExternalOutput").ap()

    def scratch(self, name, shape, dtype=F32):
        return self.nc.dram_tensor(name, list(shape), dtype, kind="Internal").ap()

    def sb(self, name, shape, dtype=F32):
        self.uid += 1
        name = "%s_%d" % (name, self.uid)
        if self.stack is None:
            return Tile(name, self.nc.alloc_sbuf_tensor(name, list(shape), dtype))
        return Tile(name, self.stack.enter_context(self.nc.sbuf_tensor(name, list(shape), dtype)))

    def ps(self, name, shape, dtype=F32):
        self.uid += 1
        name = "%s_%d" % (name, self.uid)
        nb = 4 if dtype == F32 else 2
        for d in shape[1:]:
            nb *= d
        assert nb == 2048, "PSUM tiles must be exactly one bank"
        if self.stack is None:
            return Tile(name, self.nc.alloc_psum_tensor(name, list(shape), dtype), True)
        return Tile(name, self.stack.enter_context(self.nc.psum_tensor(name, list(shape), dtype)), True)

    def stage(self):
        return _Stage(self)

    def barrier(self):
        waits_c = []
        for e in self.ENGS:
            lst = self.ops[e]
            for i in range(len(lst) - 1, -1, -1):
                if lst[i]["kind"] == "c":
                    lst[i]["marked"] = True
                    waits_c.append((("c", e), i))
                    break
        waits_d = [(("d", sk), c) for sk, c in self.dma_cnt.items()]
        for e in self.ENGS:
            wl = []
            for kk, v in waits_c + waits_d:
                if kk == ("c", e):
                    continue
                if self.waited[e].get(kk, -1) >= v:
                    continue
                self.waited[e][kk] = v
                wl.append((kk, v))
            self.ops[e].append(dict(waits=wl, fn=None, kind="n", semkey=None, marked=False))
        self.res = {}
        self.dma_map = {}

    @staticmethod
    def _key(r):
        if isinstance(r, Tile):
            return (r.name, None)
        if isinstance(r, tuple):
            a, b = r
            return (a.name if isinstance(a, Tile) else a, b)
        return (r, None)

    def _states(self, key, create):
        name, sub = key
        d = self.res.setdefault(name, {})
        if sub is None:
            if create and None not in d:
                d[None] = [None, {}]
            return list(d.values())
        out = []
        if None in d:
            out.append(d[None])
        if sub not in d and create:
            d[sub] = [None, {}]
        if sub in d:
            out.append(d[sub])
        return out

    @staticmethod
    def _evkey(ev):
        return (ev[0], ev[1]), ev[2]

    def _deps(self, r, w, ev):
        deps = {}

        def need(e):
            if e is None or e is ev:
                return
            kk, v = self._evkey(e)
            deps[kk] = max(deps.get(kk, -1), v)

        mykk, myv = self._evkey(ev)
        for x in r:
            k = self._key(x)
            for st in self._states(k, True):
                need(st[0])
        for x in w:
            k = self._key(x)
            for st in self._states(k, True):
                need(st[0])
                for kk, v in st[1].items():
                    if not (kk == mykk and v == myv):
                        deps[kk] = max(deps.get(kk, -1), v)
        for x in r:
            k = self._key(x)
            rd = self.res[k[0]][k[1]][1]
            rd[mykk] = max(rd.get(mykk, -1), myv)
        for x in w:
            name, sub = self._key(x)
            d = self.res[name]
            if sub is None:
                for s in list(d.keys()):
                    d[s] = [ev, {}]
            else:
                d[sub] = [ev, {}]
        return deps

    def _add(self, eng, fn, r, w, kind="c", semkey=None):
        r2 = []
        w = list(w)
        for x in r:
            t = x[0] if isinstance(x, tuple) else x
            if isinstance(t, Tile) and t.psum:
                w.append(t)
            else:
                r2.append(x)
        r = r2
        w = [(x[0] if (isinstance(x, tuple) and isinstance(x[0], Tile) and x[0].psum) else x) for x in w]
        lst = self.ops[eng]
        idx = len(lst)
        if kind == "d":
            cnt = self.dma_cnt.get(semkey, 0) + 1
            self.dma_cnt[semkey] = cnt
            ev = ("d", semkey, cnt)
        else:
            ev = ("c", eng, idx)
        deps = self._deps(r, w, ev)
        wl = []
        for kk, v in deps.items():
            if kk[0] == "c":
                pe = kk[1]
                if pe == eng and (eng == "tensor" or not SAME_ENGINE_SYNC):
                    continue
                self.ops[pe][v]["marked"] = True
            else:
                v = self.dma_cnt[kk[1]]
                if kind == "d" and kk[1] == semkey:
                    v -= 1
                    if v <= 0:
                        continue
            if self.waited[eng].get(kk, -1) >= v:
                continue
            self.waited[eng][kk] = v
            wl.append((kk, v))
        lst.append(dict(waits=wl, fn=fn, kind=kind, semkey=semkey, marked=False))
        return ev

    def op(self, eng, fn, r=(), w=()):
        return self._add(eng, fn, list(r), list(w))

    def dma(self, out, in_, r=(), w=(), queue="sync", key=None, **kw):
        r = list(r)
        w = list(w)
        if key is None:
            key = self._key(w[0])[0] if w else self._key(r[0])[0]
        if key not in self.dma_map:
            self.dma_map[key] = len(self.dma_map)
        key = self.dma_map[key]
        return self._add(queue, lambda e: e.dma_start(out=out, in_=in_, **kw), r, w, kind="d", semkey=key)

    def dma_split(self, out, in_, r=(), w=(), queue="sync", key=None, maxdesc=1024, **kw):
        p, a = out.shape[0], out.shape[1]
        per = max(1, maxdesc // max(1, p))
        for a0 in range(0, a, per):
            a1 = min(a, a0 + per)
            self.dma(out[:, a0:a1], in_[:, a0:a1], r=r, w=w, queue=queue, key=key, **kw)

    def idma(self, fn, r=(), w=(), key=None):
        r = list(r)
        w = list(w)
        if key is None:
            key = self._key(w[0])[0]
        if key not in self.dma_map:
            self.dma_map[key] = len(self.dma_map)
        return self._add("gpsimd", fn, r, w, kind="d", semkey=self.dma_map[key])

    def mm(self, out, lhsT, rhs, start, stop, r=(), w=()):
        return self.op("tensor", lambda e: e.matmul(out, lhsT, rhs, start=start, stop=stop), r, w)

    def act(self, out, in_, func, r=(), w=(), eng="scalar", **kw):
        return self.op(eng, lambda e: e.activation(out=out, in_=in_, func=func, **kw), r, w)

    def build(self):
        nc = self.nc
        self.barrier()
        esem = {e: nc.alloc_semaphore("sem_" + e) for e in self.ENGS}
        dsem = {sk: nc.alloc_semaphore("dsem_%d" % i) for i, sk in enumerate(self.dma_cnt)}
        import os
        OFF = int(os.environ.get("SEMOFF", "0"))
        rank = {}
        for e in self.ENGS:
            c = 0
            rk = []
            for o in self.ops[e]:
                if o["kind"] == "c" and o["marked"]:
                    c += 1
                rk.append(c)
            rank[e] = rk
        self.sem_max = {e: (rank[e][-1] if rank[e] else 0) for e in self.ENGS}
        self.dsem_max = max([16 * c for c in self.dma_cnt.values()] + [0])

        def emit(engname, eng):
            if engname == "sync" and OFF:
                for s_ in esem.values():
                    eng.sem_inc(s_, OFF)
            for o in self.ops[engname]:
                for kk, v in o["waits"]:
                    if kk[0] == "c":
                        eng.wait_ge(esem[kk[1]], rank[kk[1]][v] + OFF)
                    else:
                        eng.wait_ge(dsem[kk[1]], 16 * v)
                if o["fn"] is None:
                    continue
                ins = o["fn"](eng)
                if o["kind"] == "d":
                    ins.then_inc(dsem[o["semkey"]], 16)
                elif o["marked"]:
                    ins.then_inc(esem[engname], 1)

        with nc.Block() as block:
            @block.tensor
            def _(e):
                emit("tensor", e)

            @block.vector
            def _(e):
                emit("vector", e)

            @block.scalar
            def _(e):
                emit("scalar", e)

            @block.gpsimd
            def _(e):
                emit("gpsimd", e)

            @block.sync
            def _(e):
                emit("sync", e)
        return nc

    def run(self, in_maps):
        nc = self.build()
        res = run_bass_kernel_spmd(nc, in_maps, core_ids=list(range(len(in_maps))))
        return res.results


class Cfg:
    def __init__(self, B=4, SEQ=4096, CTX=256, D=4096, QH=16, KVH=4, GW=1024, GG=8, MW=1024, MH=4,
                 NE=16, FF=1024, GRID_W=64, DEPTH=2):
        self.B, self.SEQ, self.CTX, self.D = B, SEQ, CTX, D
        self.QH, self.KVH, self.GW, self.GG, self.MW, self.MH = QH, KVH, GW, GG, MW, MH
        self.NE, self.FF, self.GRID_W, self.DEPTH = NE, FF, GRID_W, DEPTH
        self.HD = 128
        self.KC = D // 128
        self.CT = CTX // 128
        self.LT = SEQ // 128
        self.TT = self.CT + self.LT
        self.NT = CTX + SEQ
        self.AW = QH * 128
        self.KW = KVH * 128
        o = 0
        self.off = {}
        for nm, wd in (("att_q", self.AW), ("att_k", self.KW), ("att_v", self.KW), ("gm_u", GW), ("gm_v", GW),
                       ("ml_q", MW), ("ml_k", MW), ("ml_v", MW), ("ml_o", MW), ("ml_gates", 4 * MH)):
            self.off[nm] = o
            o += wd
        self.PW = o
        self.ALPHA = (2 * DEPTH) ** 0.25
        self.EPS = 1e-6
        self.DH = MW // MH


def host_consts(cfg):
    ident = np.eye(128, dtype=np.float32)
    n = cfg.SEQ
    rows = n // cfg.GRID_W
    row = np.repeat(np.arange(rows), cfg.GRID_W).astype(np.float32)
    col = np.tile(np.arange(cfg.GRID_W), rows).astype(np.float32)
    nf = 32
    inv = (np.float32(10000.0) ** (-np.arange(nf, dtype=np.float32) / np.float32(nf))).astype(np.float32)
    ang = np.stack([row[:, None] * inv, col[:, None] * inv], axis=1).astype(np.float32)
    cs = np.stack([np.cos(ang), np.sin(ang)], axis=1).astype(np.float32)
    t = np.arange(128)
    negf = np.where(t[None, :] <= t[:, None], 0.0, -30000.0).astype(np.float32)
    negb = np.where(t[None, :] >= t[:, None], 0.0, -30000.0).astype(np.float32)
    triu_incl = (t[:, None] <= t[None, :]).astype(np.float32)
    return dict(ident=ident, rope=np.ascontiguousarray(cs.reshape(n, 128)), negf=negf, negb=negb, tri=triu_incl)


class SplitT:
    def __init__(self, aps, rows_per):
        self.aps = aps
        self.rp = rows_per

    def __getitem__(self, key):
        rs, cs = key
        s = rs.start // self.rp
        assert (rs.stop - 1) // self.rp == s
        return self.aps[s][rs.start - s * self.rp:rs.stop - s * self.rp, cs]


class Model:
    def __init__(self, cfg, debug=False):
        self.cfg = cfg
        self.debug = debug
        self.P = Prog()
        self.dumps = []

    def hbm(self, name, shape, dtype=F32, dump=False):
        if dump and self.debug:
            self.dumps.append(name)
            return self.P.out(name, shape, dtype)
        return self.P.scratch(name, shape, dtype)

    def hbm_split(self, name, cols, dtype=F32, dump=False):
        c = self.cfg
        return SplitT([self.hbm("%s_%d" % (name, s), [c.NT, cols], dtype, dump) for s in range(c.B)], c.NT)

    def declare(self):
        c, P = self.cfg, self.P
        L = c.DEPTH
        I = {}
        I["x"] = P.inp("x", [c.B, c.SEQ, c.D])
        I["c"] = P.inp("c", [c.B, c.D])
        I["ctx"] = P.inp("ctx", [c.B, c.CTX, c.D])
        I["c_ctx"] = P.inp("c_ctx", [1, c.D])
        I["w_mod"] = P.inp("w_mod", [L, c.D, 6 * c.D])
        I["b_mod"] = P.inp("b_mod", [L, 6 * c.D])
        I["w_in"] = P.inp("w_in", [L, c.D, c.PW])
        I["q_gain"] = P.inp("q_gain", [L, 128])
        I["k_gain"] = P.inp("k_gain", [L, 128])
        I["gm_ln_g"] = P.inp("gm_ln_g", [L, c.GW])
        I["gm_ln_b"] = P.inp("gm_ln_b", [L, c.GW])
        I["w_spatial"] = P.inp("w_spatial", [L, c.GG, 128, 128])
        I["b_spatial"] = P.inp("b_spatial", [L, c.GG, 128])
        I["b_gates"] = P.inp("b_gates", [L, 4 * c.MH])
        I["ml_gain"] = P.inp("ml_gain", [L, c.MW])
        I["w_out"] = P.inp("w_out", [L, c.AW + c.GW + c.MW, c.D])
        I["ln1_g"] = P.inp("ln1_g", [L, c.D])
        I["ln1_b"] = P.inp("ln1_b", [L, c.D])
        I["w_router"] = P.inp("w_router", [L, c.D, c.NE])
        I["w_e_gate"] = P.inp("w_e_gate", [L, c.NE, c.D, c.FF])
        I["w_e_up"] = P.inp("w_e_up", [L, c.NE, c.D, c.FF])
        I["w_e_down"] = P.inp("w_e_down", [L, c.NE, c.FF, c.D])
        I["ln2_g"] = P.inp("ln2_g", [L, c.D])
        I["ln2_b"] = P.inp("ln2_b", [L, c.D])
        for k, v in host_consts(c).items():
            I["k_" + k] = P.inp("k_" + k, list(v.shape))
        self.I = I
        self.ident = P.sb("ident", [128, 128])
        P.dma(self.ident[:], I["k_ident"], w=[self.ident])
        self.identb = P.sb("identb", [128, 128], BF16)
        P.op("vector", lambda e: e.tensor_copy(self.identb[:], self.ident[:]), r=[self.ident], w=[self.identb])
        self.modT = [P.sb("modT%d" % l, [128, 6, c.KC, 8]) for l in range(L)]
        self.PH = self.hbm_split("PH", c.PW, dump=True)
        self.H1 = self.hbm_split("H1", c.D, dump=True)
        self.H2 = self.hbm_split("H2", c.D, dump=True)
        self.MODR = self.hbm("MODR", [L, c.B + 1, 6 * c.D], dump=True)

    def s_mod(self, l):
        c, P, I = self.cfg, self.P, self.I
        R = c.B + 1
        KC = c.KC
        with P.stage():
            cc = P.sb("cc", [R, c.D])
            P.dma(cc[0:c.B, :], I["c"], w=[cc])
            P.dma(cc[c.B:R, :], I["c_ctx"], w=[cc])
            P.act(cc[:], cc[:], AF.Silu, r=[cc], w=[cc])
            scT = P.sb("scT", [128, KC, 8])
            pst = [P.ps("pst%d" % i, [128, 512]) for i in range(2)]
            for k in range(KC):
                ps = pst[k % 2]
                P.op("tensor", lambda e, ps=ps, k=k: e.transpose(ps[:, :R], cc[:R, k * 128:(k + 1) * 128], self.ident[:R, :R]),
                     r=[cc, self.ident], w=[ps])
                P.op("vector", lambda e, ps=ps, k=k: e.tensor_copy(scT[:, k, :R], ps[:, :R]), r=[ps], w=[(scT, k)])
            import os
            LIM = int(os.environ.get("MODLIM", "9"))
            if LIM == 0:
                return
            NCH = 512
            wsl = [P.sb("wsl%d" % i, [128, KC, NCH]) for i in range(2)]
            psm = [P.ps("psm%d" % i, [128, NCH]) for i in range(2)]
            rows = [P.sb("rows%d" % i, [R, NCH]) for i in range(2)]
            bb = [P.sb("bb%d" % i, [R, NCH]) for i in range(2)]
            mT = self.modT[l]
            P.op("vector", lambda e: e.memset(mT[:], 0.0), w=[mT])
            for ci, n0 in enumerate(range(0, 6 * c.D, NCH)):
                w_ = wsl[ci % 2]; pm = psm[ci % 2]; rw = rows[ci % 2]; b_ = bb[ci % 2]
                P.dma_split(w_[:], I["w_mod"][l, :, n0:n0 + NCH].rearrange("(kc p) n -> p kc n", p=128), w=[w_])
                P.dma(b_[:], I["b_mod"][l:l + 1, n0:n0 + NCH].broadcast_to([R, NCH]), w=[b_])
                for k in range(KC):
                    P.mm(pm[:R, :], scT[:, k, :R], w_[:, k, :], k == 0, k == KC - 1, r=[scT, w_], w=[pm])
                P.op("vector", lambda e, rw=rw, pm=pm, b_=b_: e.tensor_tensor(rw[:], pm[:R, :], b_[:], ALU.add), r=[pm, b_], w=[rw])
                P.dma(self.MODR[l, :, n0:n0 + NCH], rw[:], r=[rw])
                if LIM == 1:
                    continue
                for q in range(NCH // 128):
                    f0 = n0 + q * 128
                    j, k = f0 // c.D, (f0 % c.D) // 128
                    ps = pst[q % 2]
                    P.op("tensor", lambda e, ps=ps, rw=rw, q=q: e.transpose(ps[:, :R], rw[:R, q * 128:(q + 1) * 128], self.ident[:R, :R]),
                         r=[rw, self.ident], w=[ps])
                    P.op("vector", lambda e, ps=ps, j=j, k=k: e.tensor_copy(mT[:, j, k, :R], ps[:, :R]), r=[ps], w=[(mT, (j, k))])
            for j in (1, 4):
                P.op("vector", lambda e, j=j: e.tensor_scalar(mT[:, j, :, :], mT[:, j, :, :], 1.0, None, ALU.add), r=[mT], w=[mT])

    def h_src(self, l, s, t):
        c, I = self.cfg, self.I
        if l == 0:
            if t < c.CT:
                return I["ctx"][s, t * 128:(t + 1) * 128, :]
            return I["x"][s, (t - c.CT) * 128:(t - c.CT + 1) * 128, :]
        g = s * c.TT + t
        return self.H2[slice(g * 128, (g + 1) * 128), slice(None)]

    def ln_stats(self, x, stat, mv, rstd, D):
        P, c = self.P, self.cfg
        nch = (D + 511) // 512
        for i in range(nch):
            lo, hi = i * 512, min(D, (i + 1) * 512)
            P.op("vector", lambda e, i=i, lo=lo, hi=hi: e.bn_stats(stat[:, i, :], x[:, lo:hi]), r=[x], w=[(stat, i)])
        P.op("vector", lambda e: e.bn_aggr(mv[:], stat[:, :nch, :].rearrange("p a b -> p (a b)")), r=[stat], w=[mv])
        P.op("vector", lambda e: e.tensor_scalar_add(rstd[:], mv[:, 1:2], c.EPS), r=[mv], w=[rstd])
        P.act(rstd[:], rstd[:], AF.Sqrt, r=[rstd], w=[rstd])
        P.op("vector", lambda e: e.reciprocal(rstd[:], rstd[:]), r=[rstd], w=[rstd])

    def s_proj(self, l):
        c, P, I = self.cfg, self.P, self.I
        KC, G = c.KC, 8
        ntiles = c.B * c.TT
        import os
        ntiles = int(os.environ.get("PROJTILES", ntiles))
        mT = self.modT[l]
        with P.stage():
            aT = P.sb("aT", [128, KC, G * 128], BF16)
            xs = [P.sb("x%d" % i, [128, c.D]) for i in range(2)]
            xn = xs
            stat = [P.sb("stat%d" % i, [128, 8, 6]) for i in range(2)]
            mv = [P.sb("mv%d" % i, [128, 2]) for i in range(2)]
            rstd = [P.sb("rstd%d" % i, [128, 1]) for i in range(2)]
            pst = [P.ps("pst%d" % i, [128, 4, 128]) for i in range(2)]
            NCH = 512
            wsl = [P.sb("wsl%d" % i, [128, KC, NCH], BF16) for i in range(2)]
            psm = [P.ps("psm%d" % i, [128, NCH]) for i in range(2)]
            osb = [P.sb("osb%d" % i, [128, NCH]) for i in range(3)]
            it = 0
            wi = 0
            for g0 in range(0, ntiles, G):
                gts = list(range(g0, min(ntiles, g0 + G)))
                for gi, gt in enumerate(gts):
                    s, t = gt // c.TT, gt % c.TT
                    mrow = c.B if t < c.CT else s
                    x_, xn_, st_, mv_, rs_ = xs[gt % 2], xn[gt % 2], stat[gt % 2], mv[gt % 2], rstd[gt % 2]
                    P.dma(x_[:], self.h_src(l, s, t), w=[x_])
                    import os
                    LIM = int(os.environ.get("PROJLIM", "9"))
                    if LIM == -1:
                        continue
                    self.ln_stats(x_, st_, mv_, rs_, c.D)
                    if LIM == 0:
                        continue
                    P.op("vector", lambda e, x_=x_, xn_=xn_, mv_=mv_, rs_=rs_: e.tensor_scalar(
                        xn_[:], x_[:], mv_[:, 0:1], rs_[:, 0:1], ALU.subtract, ALU.mult), r=[x_, mv_, rs_], w=[xn_])
                    if LIM == 1:
                        continue
                    for k0 in range(0, KC, 4):
                        ps = pst[(k0 // 4) % 2]
                        for k in range(k0, min(KC, k0 + 4)):
                            P.op("tensor", lambda e, ps=ps, k=k, k0=k0, xn_=xn_: e.transpose(
                                ps[:, k - k0, :], xn_[:, k * 128:(k + 1) * 128], self.ident[:]), r=[xn_, self.ident], w=[(ps, k - k0)])
                            VAR = os.environ.get("PROJVAR", "")
                            if VAR == "imm":
                                P.act(aT[:, k, gi * 128:(gi + 1) * 128], ps[:, k - k0, :], AF.Identity,
                                      r=[(ps, k - k0), mT], w=[(aT, (k, gi))], scale=1.0, bias=0.0)
                            elif VAR == "none":
                                pass
                            elif VAR == "dve":
                                P.op("vector", lambda e, ps=ps, k=k, k0=k0, gi=gi, mrow=mrow: e.tensor_scalar(
                                    aT[:, k, gi * 128:(gi + 1) * 128], ps[:, k - k0, :], mT[:, 1, k, mrow:mrow + 1],
                                    mT[:, 0, k, mrow:mrow + 1], ALU.mult, ALU.add), r=[(ps, k - k0), mT], w=[(aT, (k, gi))])
                            else:
                                P.act(aT[:, k, gi * 128:(gi + 1) * 128], ps[:, k - k0, :], AF.Identity,
                                      r=[(ps, k - k0), mT], w=[(aT, (k, gi))],
                                      scale=mT[:, 1, k, mrow:mrow + 1], bias=mT[:, 0, k, mrow:mrow + 1])
                if LIM <= 2:
                    continue
                for n0 in range(0, c.PW, NCH):
                    nw = min(NCH, c.PW - n0)
                    w_ = wsl[wi % 2]; wi += 1
                    P.dma_split(w_[:, :, :nw], I["w_in"][l, :, n0:n0 + nw].rearrange("(kc p) n -> p kc n", p=128), w=[w_], queue="gpsimd")
                    for gi, gt in enumerate(gts):
                        pm = psm[it % 2]; o_ = osb[it % 3]; it += 1
                        for k in range(KC):
                            P.mm(pm[:, :nw], aT[:, k, gi * 128:(gi + 1) * 128], w_[:, k, :nw], k == 0, k == KC - 1,
                                 r=[(aT, (k, gi)), w_], w=[pm])
                        if it % 2:
                            P.op("vector", lambda e, o_=o_, pm=pm, nw=nw: e.tensor_copy(o_[:, :nw], pm[:, :nw]), r=[pm], w=[o_])
                        else:
                            P.act(o_[:, :nw], pm[:, :nw], AF.Copy, r=[pm], w=[o_])
                        P.dma(self.PH[gt * 128:(gt + 1) * 128, n0:n0 + nw], o_[:, :nw], r=[o_])

    def s_qk(self, l):
        c, P, I = self.cfg, self.P, self.I
        NH = c.QH + c.KVH
        ntiles = c.B * c.TT
        if not hasattr(self, "QKT"):
            self.QKT = self.hbm("QKT", [c.B, NH, 128, c.NT], BF16)
            self.MIX = self.hbm_split("MIX", c.AW + c.GW + c.MW, dump=True)
        with P.stage():
            gain = P.sb("gain", [128, NH, 128])
            for h in range(NH):
                src = I["q_gain"] if h < c.QH else I["k_gain"]
                P.dma(gain[:, h, :], src[l:l + 1, :].broadcast_to([128, 128]), w=[gain])
            xs = [P.sb("x%d" % i, [128, NH, 128]) for i in range(2)]
            sq = [P.sb("sq%d" % i, [128, NH, 128]) for i in range(2)]
            ta = [P.sb("ta%d" % i, [128, NH, 2, 32]) for i in range(2)]
            tb = [P.sb("tb%d" % i, [128, NH, 2, 32]) for i in range(2)]
            xr = [P.sb("xr%d" % i, [128, NH, 2, 2, 32]) for i in range(2)]
            ss = [P.sb("ss%d" % i, [128, NH]) for i in range(2)]
            cs = [P.sb("cs%d" % i, [128, 128]) for i in range(2)]
            stg = [P.sb("stg%d" % i, [128, NH, 512], BF16) for i in range(2)]
            pst = [P.ps("pst%d" % i, [128, 4, 128]) for i in range(2)]
            pi = 0
            for gt in range(ntiles):
                s, t = gt // c.TT, gt % c.TT
                i2 = gt % 2
                x_, sq_, ss_, xr_, ta_, tb_, cs_ = xs[i2], sq[i2], ss[i2], xr[i2], ta[i2], tb[i2], cs[i2]
                P.dma(x_[:], self.PH[gt * 128:(gt + 1) * 128, 0:NH * 128].rearrange("p (h d) -> p h d", d=128), w=[x_])
                P.op("vector", lambda e, x_=x_, sq_=sq_: e.tensor_tensor(sq_[:], x_[:], x_[:], ALU.mult), r=[x_], w=[sq_])
                P.op("vector", lambda e, ss_=ss_, sq_=sq_: e.reduce_sum(ss_[:], sq_[:], AX.X), r=[sq_], w=[ss_])
                P.op("vector", lambda e, ss_=ss_: e.tensor_scalar(ss_[:], ss_[:], 1.0 / 128, c.EPS, ALU.mult, ALU.add), r=[ss_], w=[ss_])
                P.act(ss_[:], ss_[:], AF.Sqrt, r=[ss_], w=[ss_])
                P.op("vector", lambda e, ss_=ss_: e.reciprocal(ss_[:], ss_[:]), r=[ss_], w=[ss_])
                P.op("vector", lambda e, x_=x_, sq_=sq_, ss_=ss_: e.tensor_tensor(
                    sq_[:], x_[:], ss_[:].rearrange("p (h o) -> p h o", o=1).broadcast_to([128, NH, 128]), ALU.mult), r=[x_, ss_], w=[sq_])
                xv = xr_[:].rearrange("p h a b f -> p h (a b f)")
                if t < c.CT:
                    P.op("gpsimd", lambda e, sq_=sq_, xv=xv: e.tensor_tensor(xv, sq_[:], gain[:], ALU.mult), r=[sq_, gain], w=[xr_])
                else:
                    P.op("gpsimd", lambda e, sq_=sq_: e.tensor_tensor(sq_[:], sq_[:], gain[:], ALU.mult), r=[sq_, gain], w=[sq_])
                    pos = (t - c.CT) * 128
                    P.dma(cs_[:], I["k_rope"][pos:pos + 128, :], w=[cs_])
                    v = sq_[:].rearrange("p h (a b f) -> p h a b f", a=2, b=2)
                    t1, t2 = v[:, :, :, 0, :], v[:, :, :, 1, :]
                    cosb = cs_[:, 0:64].rearrange("p (a f) -> p a f", a=2)[:, None, :, :].broadcast_to([128, NH, 2, 32])
                    sinb = cs_[:, 64:128].rearrange("p (a f) -> p a f", a=2)[:, None, :, :].broadcast_to([128, NH, 2, 32])
                    P.op("vector", lambda e, ta_=ta_, t1=t1, cosb=cosb: e.tensor_tensor(ta_[:], t1, cosb, ALU.mult), r=[sq_, cs_], w=[ta_])
                    P.op("gpsimd", lambda e, tb_=tb_, t2=t2, sinb=sinb: e.tensor_tensor(tb_[:], t2, sinb, ALU.mult), r=[sq_, cs_], w=[tb_])
                    P.op("vector", lambda e, ta_=ta_, tb_=tb_, xr_=xr_: e.tensor_tensor(xr_[:, :, :, 0, :], ta_[:], tb_[:], ALU.subtract), r=[ta_, tb_], w=[(xr_, 0)])
                    P.op("vector", lambda e, ta_=ta_, t2=t2, cosb=cosb: e.tensor_tensor(ta_[:], t2, cosb, ALU.mult), r=[sq_, cs_], w=[ta_])
                    P.op("gpsimd", lambda e, tb_=tb_, t1=t1, sinb=sinb: e.tensor_tensor(tb_[:], t1, sinb, ALU.mult), r=[sq_, cs_], w=[tb_])
                    P.op("vector", lambda e, ta_=ta_, tb_=tb_, xr_=xr_: e.tensor_tensor(xr_[:, :, :, 1, :], ta_[:], tb_[:], ALU.add), r=[ta_, tb_], w=[(xr_, 1)])
                grp = t // 4
                sg = stg[(s * ((c.TT + 3) // 4) + grp) % 2]
                tpos = t % 4
                for h0 in range(0, NH, 4):
                    ps = pst[pi % 2]; pi += 1
                    hs = list(range(h0, min(NH, h0 + 4)))
                    for h in hs:
                        P.op("tensor", lambda e, ps=ps, h=h, h0=h0, xv=xv: e.transpose(ps[:, h - h0, :], xv[:, h, :], self.ident[:]),
                             r=[xr_, self.ident], w=[ps])
                    n = len(hs)
                    if (h0 // 4) % 2 == 0:
                        P.op("vector", lambda e, ps=ps, sg=sg, h0=h0, n=n, tpos=tpos: e.tensor_copy(
                            sg[:, h0:h0 + n, tpos * 128:(tpos + 1) * 128], ps[:, 0:n, :]), r=[ps], w=[(sg, tpos)])
                    else:
                        P.act(sg[:, h0:h0 + n, tpos * 128:(tpos + 1) * 128], ps[:, 0:n, :], AF.Copy, r=[ps], w=[(sg, tpos)])
                if tpos == 3 or t == c.TT - 1:
                    t0 = grp * 4
                    nt = t - t0 + 1
                    P.dma_split(self.QKT[s, :, :, t0 * 128:(t0 + nt) * 128].rearrange("h d n -> d h n"), sg[:, :, 0:nt * 128], r=[sg], key=sg.name)

    def s_attn(self, l):
        c, P, I = self.cfg, self.P, self.I
        last = l == c.DEPTH - 1
        G = c.QH // c.KVH
        with P.stage():
            gq = P.sb("gq", [128, 128]); gk = P.sb("gk", [128, 128])
            P.dma(gq[:], I["q_gain"][l:l + 1, :].broadcast_to([128, 128]), w=[gq])
            P.dma(gk[:], I["k_gain"][l:l + 1, :].broadcast_to([128, 128]), w=[gk])
            mq = P.sb("mq", [128, 4]); nb = P.sb("nb", [128, 4])
            P.op("vector", lambda e: e.tensor_tensor(gq[:], gq[:], gq[:], ALU.mult), r=[gq], w=[gq])
            P.op("vector", lambda e: e.tensor_tensor(gk[:], gk[:], gk[:], ALU.mult), r=[gk], w=[gk])
            P.op("vector", lambda e: e.reduce_max(mq[:, 0:1], gq[:], AX.X), r=[gq], w=[(mq, 0)])
            P.op("vector", lambda e: e.reduce_max(mq[:, 1:2], gk[:], AX.X), r=[gk], w=[(mq, 1)])
            P.op("vector", lambda e: e.tensor_tensor(nb[:, 0:1], mq[:, 0:1], mq[:, 1:2], ALU.mult), r=[mq], w=[nb])
            P.act(nb[:, 0:1], nb[:, 0:1], AF.Sqrt, r=[nb], w=[nb])
            P.op("vector", lambda e: e.tensor_scalar(nb[:, 0:1], nb[:, 0:1], -float(np.sqrt(128.0)), None, ALU.mult), r=[nb], w=[nb])
            scale = float(128.0 ** -0.5)
            kT = [P.sb("kT%d" % i, [128, c.NT], BF16) for i in range(2)]
            va = [P.sb("va%d" % i, [128, c.TT, 132], BF16) for i in range(2)]
            qT = [P.sb("qT%d" % i, [128, c.NT], BF16) for i in range(2)]
            pS = [P.ps("pS%d" % i, [128, 512]) for i in range(2)]
            acc = [P.ps("acc%d" % i, [128, 512]) for i in range(4)]
            pT = [P.sb("pT%d" % i, [128, 512], BF16) for i in range(3)]
            ob = [P.sb("ob%d" % i, [128, 128]) for i in range(4)]
            rc = [P.sb("rc%d" % i, [128, 4]) for i in range(4)]
            si = 0; pi = 0; hi = 0; kvi = 0
            for s in range(c.B):
                for kvh in range(c.KVH):
                    kT_, va_ = kT[kvi % 2], va[kvi % 2]; kvi += 1
                    P.dma(kT_[:], self.QKT[s, c.QH + kvh, :, :], w=[kT_])
                    P.op("vector", lambda e, va_=va_: e.memset(va_[:, :, 128:132], 1.0), w=[(va_, "ones")])
                    voff = c.off["att_v"] + kvh * 128
                    P.dma_split(va_[:, :, 0:128], self.PH[slice(s * c.NT, (s + 1) * c.NT), slice(voff, voff + 128)].rearrange("(t p) d -> p t d", p=128),
                                w=[(va_, "v")], queue="gpsimd")
                    for g in range(G):
                        h = kvh * G + g
                        qT_ = qT[hi % 2]; hi += 1
                        P.dma(qT_[:], self.QKT[s, h, :, :], w=[qT_])
                        chunks = []
                        if not last:
                            chunks.append((0, c.CTX, c.CT))
                        for q0 in range(c.CTX, c.NT, 512):
                            chunks.append((q0, min(512, c.NT - q0), c.TT))
                        for (q0, qn, nkb) in chunks:
                            nqs = qn // 128
                            for kb in range(nkb):
                                ps = pS[si % 2]; si += 1
                                P.mm(ps[:, :qn], kT_[:, kb * 128:(kb + 1) * 128], qT_[:, q0:q0 + qn], True, True, r=[kT_, qT_], w=[ps])
                                pt = pT[pi % 3]; pi += 1
                                P.act(pt[:, :qn], ps[:, :qn], AF.Exp, r=[ps, nb], w=[pt], scale=scale, bias=nb[:, 0:1])
                                for qs in range(nqs):
                                    P.mm(acc[qs][:, 0:129], pt[:, qs * 128:(qs + 1) * 128], va_[:, kb, 0:129], kb == 0, kb == nkb - 1,
                                         r=[pt, va_], w=[acc[qs]])
                            for qs in range(nqs):
                                o_, r_ = ob[qs], rc[qs]
                                P.op("vector", lambda e, r_=r_, a=acc[qs]: e.reciprocal(r_[:, 0:1], a[:, 128:129]), r=[acc[qs]], w=[r_])
                                P.op("vector", lambda e, o_=o_, r_=r_, a=acc[qs]: e.tensor_scalar(o_[:], a[:, 0:128], r_[:, 0:1], None, ALU.mult),
                                     r=[acc[qs], r_], w=[o_])
                                row = s * c.NT + q0 + qs * 128
                                P.dma(self.MIX[row:row + 128, h * 128:(h + 1) * 128], o_[:], r=[o_])

    def s_gmlp(self, l):
        c, P, I = self.cfg, self.P, self.I
        GW, GG = c.GW, c.GG
        ntiles = c.B * c.TT
        with P.stage():
            lng = P.sb("lng", [128, GW]); lnb = P.sb("lnb", [128, GW])
            P.dma(lng[:], I["gm_ln_g"][l:l + 1, :].broadcast_to([128, GW]), w=[lng])
            P.dma(lnb[:], I["gm_ln_b"][l:l + 1, :].broadcast_to([128, GW]), w=[lnb])
            wsT = P.sb("wsT", [128, GG, 128])
            wtmp = P.sb("wtmp", [128, GG, 128])
            P.dma(wtmp[:], I["w_spatial"][l].rearrange("g p q -> p g q"), w=[wtmp])
            bs = P.sb("bs", [GG, 128]); bsT = P.sb("bsT", [128, 8])
            P.dma(bs[:], I["b_spatial"][l], w=[bs])
            pst = [P.ps("pst%d" % i, [128, 4, 128]) for i in range(4)]
            for g0 in range(0, GG, 4):
                ps = pst[(g0 // 4) % 2]
                n = min(4, GG - g0)
                for g in range(g0, g0 + n):
                    P.op("tensor", lambda e, ps=ps, g=g, g0=g0: e.transpose(ps[:, g - g0, :], wtmp[:, g, :], self.ident[:]), r=[wtmp, self.ident], w=[ps])
                P.op("vector", lambda e, ps=ps, g0=g0, n=n: e.tensor_copy(wsT[:, g0:g0 + n, :], ps[:, 0:n, :]), r=[ps], w=[wsT])
            ps = pst[0]
            P.op("tensor", lambda e: e.transpose(ps[:, 0, 0:GG], bs[:GG, :], self.ident[:GG, :GG]), r=[bs, self.ident], w=[ps])
            P.op("vector", lambda e: e.tensor_copy(bsT[:, 0:GG], ps[:, 0, 0:GG]), r=[ps], w=[bsT])
            uv = [P.sb("uv%d" % i, [128, 2 * GW]) for i in range(2)]
            t1 = [P.sb("t1%d" % i, [128, 2 * GW]) for i in range(2)]
            gl = [P.sb("gl%d" % i, [128, 2 * GW]) for i in range(2)]
            vn = [P.sb("vn%d" % i, [128, GW]) for i in range(2)]
            og = [P.sb("og%d" % i, [128, GW]) for i in range(2)]
            stat = [P.sb("stat%d" % i, [128, 8, 6]) for i in range(2)]
            mv = [P.sb("mv%d" % i, [128, 2]) for i in range(2)]
            rstd = [P.sb("rstd%d" % i, [128, 1]) for i in range(2)]
            pi = 0
            uo = c.off["gm_u"]
            import os
            GLIM = int(os.environ.get("GLIM", "9"))
            if GLIM == 0:
                return
            for gt in range(ntiles):
                i2 = gt % 2
                uv_, t1_, gl_, vn_, og_ = uv[i2], t1[i2], gl[i2], vn[i2], og[i2]
                P.dma(uv_[:], self.PH[gt * 128:(gt + 1) * 128, uo:uo + 2 * GW], w=[uv_])
                P.op("vector", lambda e, uv_=uv_, t1_=t1_: e.tensor_tensor(t1_[:], uv_[:], uv_[:], ALU.mult), r=[uv_], w=[t1_])
                P.op("vector", lambda e, t1_=t1_: e.tensor_scalar(t1_[:], t1_[:], 0.044715, 1.0, ALU.mult, ALU.add), r=[t1_], w=[t1_])
                P.op("gpsimd", lambda e, uv_=uv_, t1_=t1_: e.tensor_tensor(t1_[:], t1_[:], uv_[:], ALU.mult), r=[uv_, t1_], w=[t1_])
                P.act(t1_[:], t1_[:], AF.Sigmoid, r=[t1_], w=[t1_], scale=1.5957691216057308)
                P.op("vector", lambda e, uv_=uv_, t1_=t1_, gl_=gl_: e.tensor_tensor(gl_[:], t1_[:], uv_[:], ALU.mult), r=[uv_, t1_], w=[gl_])
                if GLIM == 1:
                    continue
                vsl = gl_[:, GW:2 * GW]
                st_, mv_, rs_ = stat[i2], mv[i2], rstd[i2]
                nch = (GW + 511) // 512
                for i in range(nch):
                    lo, hi = i * 512, min(GW, (i + 1) * 512)
                    P.op("vector", lambda e, i=i, lo=lo, hi=hi, st_=st_, gl_=gl_: e.bn_stats(st_[:, i, :], gl_[:, GW + lo:GW + hi]), r=[gl_], w=[(st_, i)])
                P.op("vector", lambda e, st_=st_, mv_=mv_: e.bn_aggr(mv_[:], st_[:, :nch, :].rearrange("p a b -> p (a b)")), r=[st_], w=[mv_])
                P.op("vector", lambda e, mv_=mv_, rs_=rs_: e.tensor_scalar_add(rs_[:], mv_[:, 1:2], c.EPS), r=[mv_], w=[rs_])
                P.act(rs_[:], rs_[:], AF.Sqrt, r=[rs_], w=[rs_])
                P.op("vector", lambda e, rs_=rs_: e.reciprocal(rs_[:], rs_[:]), r=[rs_], w=[rs_])
                P.op("vector", lambda e, vn_=vn_, vsl=vsl, mv_=mv_, rs_=rs_: e.tensor_scalar(vn_[:], vsl, mv_[:, 0:1], rs_[:, 0:1], ALU.subtract, ALU.mult),
                     r=[gl_, mv_, rs_], w=[vn_])
                P.op("gpsimd", lambda e, vn_=vn_: e.tensor_tensor(vn_[:], vn_[:], lng[:], ALU.mult), r=[vn_, lng], w=[vn_])
                P.op("gpsimd", lambda e, vn_=vn_: e.tensor_tensor(vn_[:], vn_[:], lnb[:], ALU.add), r=[vn_, lnb], w=[vn_])
                if GLIM == 2:
                    continue
                for g0 in range(0, GG, 2):
                    ps = pst[pi % 4]; pi += 1
                    n = min(2, GG - g0)
                    for g in range(g0, g0 + n):
                        P.mm(ps[:, g - g0, :], wsT[:, g, :], vn_[:, g * 128:(g + 1) * 128], True, True, r=[wsT, vn_], w=[ps])
                    if GLIM == 3:
                        continue
                    for g in range(g0, g0 + n):
                        P.op("vector", lambda e, ps=ps, g=g, g0=g0, og_=og_, gl_=gl_: e.scalar_tensor_tensor(
                            og_[:, g * 128:(g + 1) * 128], ps[:, g - g0, :], bsT[:, g:g + 1], gl_[:, g * 128:(g + 1) * 128], ALU.add, ALU.mult),
                            r=[ps, bsT, gl_], w=[(og_, g)])
                if GLIM in (3, 4):
                    continue
                P.dma(self.MIX[gt * 128:(gt + 1) * 128, c.AW:c.AW + GW], og_[:], r=[og_])

    def s_gates(self, l):
        c, P, I = self.cfg, self.P, self.I
        MH, TT, NT, CT = c.MH, c.TT, c.NT, c.CT
        R = c.B * MH
        if not hasattr(self, "COLS"):
            self.COLS = [P.sb("cols%d" % d, [128, 4, TT, R]) for d in range(2)]
            self.UROW = self.hbm("UROW", [2, R, NT])
            self.WPR = self.hbm("WPR", [2, R, TT])
            self.GTH = self.hbm("GTH", [c.B, 4 * MH, NT])
            self.HS = [self.hbm_split("HS%d" % d, c.MW, dump=True) for d in range(2)]
        go = c.off["ml_gates"]
        with P.stage():
            g16 = [P.sb("g16%d" % i, [128, 4 * MH]) for i in range(2)]
            gT = [P.sb("gT%d" % i, [4 * MH, NT]) for i in range(1)]
            pst = [P.ps("pst%d" % i, [128, 4, 128]) for i in range(2)]
            pi = 0
            for s in range(c.B):
                gT_ = gT[0]
                for t in range(TT):
                    gt = s * TT + t
                    g_ = g16[gt % 2]
                    P.dma(g_[:], self.PH[gt * 128:(gt + 1) * 128, go:go + 4 * MH], w=[g_])
                    if t % 4 == 0:
                        ps = pst[pi % 2]; pi += 1
                    P.mm(ps[:4 * MH, t % 4, :], g_[:, :], self.ident[:], True, True, r=[g_, self.ident], w=[ps])
                    if t % 4 == 3 or t == TT - 1:
                        t0 = (t // 4) * 4
                        n = t - t0 + 1
                        P.op("vector", lambda e, ps=ps, gT_=gT_, t0=t0, n=n: e.tensor_copy(
                            gT_[:, t0 * 128:(t0 + n) * 128].rearrange("r (t p) -> r t p", p=128), ps[:4 * MH, 0:n, :]), r=[ps], w=[gT_])
                P.dma(self.GTH[s], gT_[:], r=[gT_], w=["GTH"])
        NP = 2 if TT % 2 == 0 else 1
        TP = TT // NP
        NTP = TP * 128
        with P.stage():
            bias = [P.sb("bias%d" % j, [R, 1]) for j in range(4)]
            for j in range(4):
                for s in range(c.B):
                    P.dma(bias[j][s * MH:(s + 1) * MH, :], I["b_gates"][l, j * MH:(j + 1) * MH].rearrange("(h o) -> h o", o=1), w=[bias[j]])
            BC = P.sb("BC", [R, NT]); U = P.sb("U", [R, NT]); Gg = P.sb("Gg", [R, NT]); CM = P.sb("CM", [R, NT])
            T4 = [P.sb("T%d" % i, [R, NTP]) for i in range(4)]
            bl = P.sb("bl", [R, TT]); gmx = P.sb("gmx", [R, TT]); mprev = P.sb("mprev", [R, TT]); mnew = P.sb("mnew", [R, TT])
            wpv = P.sb("wpv", [R, TT]); tmp = P.sb("tmp", [R, 4])
            pst = [P.ps("pst%d" % i, [128, 512]) for i in range(2)]
            pi = 0

            def v3(ap):
                return ap.rearrange("r (t p) -> r t p", p=128)

            def scan128(dst_ap, src_t, tA, tB, op, d):
                cur_ap, cur_t = src_t[:], src_t
                k = 1
                step = 0
                while k < 128:
                    lastk = k * 2 >= 128
                    nxt_t = None if lastk else (tA if step % 2 == 0 else tB)
                    nxt_ap = dst_ap if lastk else nxt_t[:]
                    cv, nv = v3(cur_ap), v3(nxt_ap)
                    wres = [dst_res] if lastk else [nxt_t]
                    if d == 0:
                        P.op("vector", lambda e, cv=cv, nv=nv, k=k: e.tensor_tensor(nv[:, :, k:], cv[:, :, k:], cv[:, :, :128 - k], op), r=[cur_t], w=wres)
                        P.op("gpsimd", lambda e, cv=cv, nv=nv, k=k: e.tensor_copy(nv[:, :, :k], cv[:, :, :k]), r=[cur_t], w=wres)
                    else:
                        P.op("vector", lambda e, cv=cv, nv=nv, k=k: e.tensor_tensor(nv[:, :, :128 - k], cv[:, :, :128 - k], cv[:, :, k:], op), r=[cur_t], w=wres)
                        P.op("gpsimd", lambda e, cv=cv, nv=nv, k=k: e.tensor_copy(nv[:, :, 128 - k:], cv[:, :, 128 - k:]), r=[cur_t], w=wres)
                    cur_ap, cur_t = nxt_ap, (nxt_t if nxt_t is not None else dst_res)
                    k *= 2
                    step += 1

            for d in range(2):
                bi, bf = bias[2 * d], bias[2 * d + 1]
                for pc in range(NP):
                    cols_ = slice(pc * NTP, (pc + 1) * NTP)
                    tsl = slice(pc * TP, (pc + 1) * TP)
                    IC, LF, tA, tB = T4
                    for s in range(c.B):
                        P.dma(IC[s * MH:(s + 1) * MH, :], self.GTH[s, 2 * d * MH:(2 * d + 1) * MH, cols_], r=["GTH"], w=[IC])
                        P.dma(LF[s * MH:(s + 1) * MH, :], self.GTH[s, (2 * d + 1) * MH:(2 * d + 2) * MH, cols_], r=["GTH"], w=[LF])
                    P.op("vector", lambda e, IC=IC, bi=bi: e.tensor_scalar(IC[:], IC[:], bi[:, 0:1], None, ALU.add), r=[IC, bi], w=[IC])
                    P.op("vector", lambda e, LF=LF, bf=bf: e.tensor_scalar(LF[:], LF[:], bf[:, 0:1], None, ALU.add), r=[LF, bf], w=[LF])
                    P.op("vector", lambda e, LF=LF, tA=tA: e.tensor_scalar(tA[:], LF[:], -1.0, None, ALU.mult), r=[LF], w=[tA])
                    P.op("vector", lambda e, LF=LF, tA=tA: e.tensor_tensor(tA[:], tA[:], LF[:], ALU.max), r=[tA, LF], w=[tA])
                    P.act(tA[:], tA[:], AF.Exp, r=[tA], w=[tA], scale=-1.0)
                    P.act(tA[:], tA[:], AF.Ln, r=[tA], w=[tA], bias=1.0)
                    P.op("vector", lambda e, LF=LF: e.tensor_scalar(LF[:], LF[:], 0.0, None, ALU.min), r=[LF], w=[LF])
                    P.op("vector", lambda e, LF=LF, tA=tA: e.tensor_tensor(LF[:], LF[:], tA[:], ALU.subtract), r=[LF, tA], w=[LF])
                    dst_res = BC
                    scan128(BC[:, cols_], LF, tA, tB, ALU.add, d)
                    edge = 127 if d == 0 else 0
                    P.op("vector", lambda e, edge=edge, cols_=cols_, tsl=tsl: e.tensor_copy(bl[:, tsl], v3(BC[:, cols_])[:, :, edge]), r=[BC], w=[bl])
                    P.op("vector", lambda e, IC=IC, cols_=cols_: e.tensor_tensor(U[:, cols_], IC[:], BC[:, cols_], ALU.subtract), r=[IC, BC], w=[U])
                    P.op("vector", lambda e, cols_=cols_, tsl=tsl: e.tensor_tensor(
                        v3(Gg[:, cols_]), v3(U[:, cols_]), bl[:, tsl].rearrange("r (t o) -> r t o", o=1).broadcast_to([R, TP, 128]), ALU.add), r=[U, bl], w=[Gg])
                    P.op("vector", lambda e, cols_=cols_, tsl=tsl: e.reduce_max(gmx[:, tsl], v3(Gg[:, cols_]), AX.X), r=[Gg], w=[gmx])
                    P.op("gpsimd", lambda e, IC=IC, cols_=cols_: e.tensor_copy(IC[:], U[:, cols_]), r=[U], w=[IC])
                    dst_res = CM
                    scan128(CM[:, cols_], IC, tA, tB, ALU.max, d)
                order = list(range(TT)) if d == 0 else (list(range(CT - 1, -1, -1)) + list(range(TT - 1, CT - 1, -1)))
                prev = None
                for cc in order:
                    if prev is None:
                        P.op("vector", lambda e, cc=cc: e.memset(mprev[:, cc:cc + 1], 0.0), w=[mprev])
                    else:
                        P.op("vector", lambda e, cc=cc, prev=prev: e.tensor_copy(mprev[:, cc:cc + 1], mnew[:, prev:prev + 1]), r=[mnew], w=[mprev])
                    P.op("vector", lambda e, cc=cc: e.tensor_tensor(tmp[:, 0:1], bl[:, cc:cc + 1], mprev[:, cc:cc + 1], ALU.add), r=[bl, mprev], w=[tmp])
                    P.op("vector", lambda e, cc=cc: e.tensor_tensor(mnew[:, cc:cc + 1], tmp[:, 0:1], gmx[:, cc:cc + 1], ALU.max), r=[tmp, gmx], w=[mnew])
                    prev = cc
                P.op("vector", lambda e: e.tensor_tensor(wpv[:], bl[:], mprev[:], ALU.add), r=[bl, mprev], w=[wpv])
                P.op("vector", lambda e: e.tensor_tensor(wpv[:], wpv[:], mnew[:], ALU.subtract), r=[wpv, mnew], w=[wpv])
                P.act(wpv[:], wpv[:], AF.Exp, r=[wpv], w=[wpv])
                P.dma(self.UROW[d], U[:], r=[U], w=["UROW"])
                P.dma(self.WPR[d], wpv[:], r=[wpv], w=["WPR"])
                cols = self.COLS[d]
                for pc in range(NP):
                    cols_ = slice(pc * NTP, (pc + 1) * NTP)
                    tsl = slice(pc * TP, (pc + 1) * TP)
                    Q = T4
                    a_, mt_ = Q[0], Q[1]
                    bcp = lambda t_, tsl=tsl: t_[:, tsl].rearrange("r (t o) -> r t o", o=1).broadcast_to([R, TP, 128])
                    P.op("vector", lambda e, cols_=cols_, bcp=bcp, a_=a_: e.tensor_tensor(v3(a_[:]), v3(BC[:, cols_]), bcp(mprev), ALU.add), r=[BC, mprev], w=[a_])
                    P.op("vector", lambda e, cols_=cols_, mt_=mt_: e.tensor_tensor(mt_[:], BC[:, cols_], CM[:, cols_], ALU.add), r=[BC, CM], w=[mt_])
                    P.op("vector", lambda e, a_=a_, mt_=mt_: e.tensor_tensor(mt_[:], mt_[:], a_[:], ALU.max), r=[mt_, a_], w=[mt_])
                    P.op("vector", lambda e, a_=a_, mt_=mt_: e.tensor_tensor(a_[:], a_[:], mt_[:], ALU.subtract), r=[a_, mt_], w=[a_])
                    P.act(a_[:], a_[:], AF.Exp, r=[a_], w=[a_])
                    P.op("vector", lambda e, cols_=cols_, mt_=mt_, q2=Q[2]: e.tensor_tensor(q2[:], BC[:, cols_], mt_[:], ALU.subtract), r=[BC, mt_], w=[Q[2]])
                    P.act(mt_[:], mt_[:], AF.Exp, r=[mt_], w=[mt_], scale=-1.0)
                    P.op("vector", lambda e, cols_=cols_, bcp=bcp, q3=Q[3]: e.tensor_tensor(v3(q3[:]), v3(Gg[:, cols_]), bcp(mnew), ALU.subtract), r=[Gg, mnew], w=[Q[3]])
                    P.act(Q[3][:], Q[3][:], AF.Exp, r=[Q[3]], w=[Q[3]])
                    for qi, src_ in enumerate((Q[2], Q[0], Q[1], Q[3])):
                        per = max(1, 512 // R)
                        for t0 in range(0, TP, per):
                            ps = pst[pi % 2]; pi += 1
                            n = min(per, TP - t0)
                            for t in range(t0, t0 + n):
                                P.op("tensor", lambda e, ps=ps, t=t, t0=t0, src_=src_: e.transpose(
                                    ps[:, (t - t0) * R:(t - t0 + 1) * R], src_[:R, t * 128:(t + 1) * 128], self.ident[:R, :R]), r=[src_, self.ident], w=[ps])
                            P.op("vector", lambda e, ps=ps, t0=t0, n=n, qi=qi, cols=cols, pc=pc: e.tensor_copy(
                                cols[:, qi, pc * TP + t0:pc * TP + t0 + n, :], ps[:, 0:n * R].rearrange("p (t r) -> p t r", r=R)), r=[ps], w=[cols])

    def s_mlstm(self, l):
        c, P, I = self.cfg, self.P, self.I
        MH, TT, NT, CT, DH = c.MH, c.TT, c.NT, c.CT, c.DH
        DC = DH // 128
        DA = DH + 4
        R = c.B * MH
        with P.stage():
            negm = [P.sb("negm%d" % d, [128, 128]) for d in range(2)]
            P.dma(negm[0][:], I["k_negf"], w=[negm[0]])
            P.dma(negm[1][:], I["k_negb"], w=[negm[1]])
            qst = [P.sb("qst%d" % i, [128, DH]) for i in range(2)]
            ks = P.sb("ks", [128, TT, DH]); va = P.sb("va", [128, TT, DA])
            qT = P.sb("qT", [128, DC, NT]); kT = P.sb("kT", [128, DC, NT])
            ub = P.sb("ub", [128, NT]); wpb = P.sb("wpb", [128, TT])
            CA = P.sb("CA", [128, DC, DA])
            arg = [P.sb("arg%d" % i, [128, 128]) for i in range(2)]
            Dm = [P.sb("Dm%d" % i, [128, 128]) for i in range(2)]
            Sd = [P.sb("Sd%d" % i, [128, 128]) for i in range(2)]
            SdT = [P.sb("SdT%d" % i, [128, 128]) for i in range(2)]
            Bsb = [P.sb("Bsb%d" % i, [128, DA]) for i in range(2)]
            num = [P.sb("num%d" % i, [128, DA]) for i in range(2)]
            dn = [P.sb("dn%d" % i, [128, 4]) for i in range(2)]
            ho = [P.sb("ho%d" % i, [128, DH]) for i in range(2)]
            kw = [P.sb("kw%d" % i, [128, DH]) for i in range(2)]
            pS = P.ps("pS", [128, 512]); pT = P.ps("pT", [128, 512])
            pA = P.ps("pA", [128, 512]); pB = P.ps("pB", [128, 512])
            pU = [P.ps("pU%d" % i, [128, 512]) for i in range(2)]
            ptr = [P.ps("ptr%d" % i, [128, 4, 128]) for i in range(2)]
            it = 0
            tri = 0
            for s in range(c.B):
                for h in range(MH):
                    rho = s * MH + h
                    rows = slice(s * NT, (s + 1) * NT)
                    o = c.off["ml_k"] + h * DH
                    P.dma_split(ks[:], self.PH[rows, slice(o, o + DH)].rearrange("(t p) d -> p t d", p=128), w=[ks])
                    o = c.off["ml_v"] + h * DH
                    P.dma_split(va[:, :, 0:DH], self.PH[rows, slice(o, o + DH)].rearrange("(t p) d -> p t d", p=128), w=[(va, "v")])
                    P.op("vector", lambda e: e.memset(va[:, :, DH:DA], 1.0), w=[(va, "o")])
                    oq = c.off["ml_q"] + h * DH
                    for (isq, dstT) in ((True, qT), (False, kT)):
                        for t in range(TT):
                            if isq:
                                q_ = qst[t % 2]
                                P.dma(q_[:], self.PH[slice(s * NT + t * 128, s * NT + (t + 1) * 128), slice(oq, oq + DH)], w=[q_])
                            for dc in range(DC):
                                j = t * DC + dc
                                if j % 4 == 0:
                                    ps = ptr[tri % 2]; tri += 1
                                    j0 = j
                                if isq:
                                    P.op("tensor", lambda e, ps=ps, j=j, j0=j0, q_=q_, dc=dc: e.transpose(
                                        ps[:, j - j0, :], q_[:, dc * 128:(dc + 1) * 128], self.ident[:]), r=[q_, self.ident], w=[ps])
                                else:
                                    P.op("tensor", lambda e, ps=ps, j=j, j0=j0, t=t, dc=dc: e.transpose(
                                        ps[:, j - j0, :], ks[:, t, dc * 128:(dc + 1) * 128], self.ident[:]), r=[ks, self.ident], w=[ps])
                                if j % 4 == 3 or j == TT * DC - 1:
                                    for jj in range(j0, j + 1):
                                        tt, dd = jj // DC, jj % DC
                                        if jj % 2:
                                            P.op("vector", lambda e, ps=ps, jj=jj, j0=j0, tt=tt, dd=dd, dstT=dstT: e.tensor_copy(
                                                dstT[:, dd, tt * 128:(tt + 1) * 128], ps[:, jj - j0, :]), r=[ps], w=[(dstT, jj)])
                                        else:
                                            P.act(dstT[:, dd, tt * 128:(tt + 1) * 128], ps[:, jj - j0, :], AF.Copy, r=[ps], w=[(dstT, jj)])
                    for d in range(2):
                        cols = self.COLS[d]
                        P.dma(ub[:], self.UROW[d, rho:rho + 1, :].broadcast_to([128, NT]), r=["UROW"], w=[ub])
                        P.dma(wpb[:], self.WPR[d, rho:rho + 1, :].broadcast_to([128, TT]), r=["WPR"], w=[wpb])
                        P.op("vector", lambda e: e.memset(CA[:], 0.0), w=[CA])
                        order = list(range(TT)) if d == 0 else (list(range(CT - 1, -1, -1)) + list(range(TT - 1, CT - 1, -1)))
                        for cc in order:
                            i2 = it % 2; it += 1
                            cs = slice(cc * 128, (cc + 1) * 128)
                            arg_, Dm_, Sd_, SdT_, Bsb_, num_, dn_, ho_, kw_ = arg[i2], Dm[i2], Sd[i2], SdT[i2], Bsb[i2], num[i2], dn[i2], ho[i2], kw[i2]
                            for dc in range(DC):
                                P.mm(pS[:, 0:128], qT[:, dc, cs], kT[:, dc, cs], dc == 0, dc == DC - 1, r=[qT, kT], w=[pS])
                            P.op("gpsimd", lambda e, arg_=arg_, cs=cs, d=d: e.tensor_tensor(arg_[:], ub[:, cs], negm[d][:], ALU.add), r=[ub, negm[d]], w=[arg_])
                            P.act(Dm_[:], arg_[:], AF.Exp, r=[arg_, cols], w=[Dm_], bias=cols[:, 0, cc, rho:rho + 1])
                            P.op("vector", lambda e, Sd_=Sd_, Dm_=Dm_: e.scalar_tensor_tensor(Sd_[:], pS[:, 0:128], 1.0 / 16, Dm_[:], ALU.mult, ALU.mult),
                                 r=[pS, Dm_], w=[Sd_])
                            P.op("tensor", lambda e, Sd_=Sd_: e.transpose(pT[:, 0:128], Sd_[:], self.ident[:]), r=[Sd_, self.ident], w=[pT])
                            P.act(SdT_[:], pT[:, 0:128], AF.Copy, r=[pT], w=[SdT_])
                            for dc in range(DC):
                                P.mm(pA[:, 0:DH + 1], qT[:, dc, cs], CA[:, dc, 0:DH + 1], dc == 0, dc == DC - 1, r=[qT, CA], w=[pA])
                            P.mm(pB[:, 0:DH + 1], SdT_[:], va[:, cc, 0:DH + 1], True, True, r=[SdT_, va], w=[pB])
                            P.act(Bsb_[:, 0:DH + 1], pB[:, 0:DH + 1], AF.Copy, r=[pB], w=[Bsb_])
                            P.op("vector", lambda e, num_=num_, Bsb_=Bsb_, cc=cc, cols=cols, rho=rho: e.scalar_tensor_tensor(
                                num_[:, 0:DH + 1], pA[:, 0:DH + 1], cols[:, 1, cc, rho:rho + 1], Bsb_[:, 0:DH + 1], ALU.mult, ALU.add),
                                r=[pA, Bsb_, cols], w=[num_])
                            P.op("vector", lambda e, dn_=dn_, num_=num_: e.tensor_scalar(dn_[:, 0:1], num_[:, DH:DH + 1], -1.0, None, ALU.mult), r=[num_], w=[dn_])
                            P.op("vector", lambda e, dn_=dn_, num_=num_: e.tensor_tensor(dn_[:, 0:1], dn_[:, 0:1], num_[:, DH:DH + 1], ALU.max), r=[num_, dn_], w=[dn_])
                            P.op("vector", lambda e, dn_=dn_, cc=cc, cols=cols, rho=rho: e.tensor_tensor(dn_[:, 0:1], dn_[:, 0:1], cols[:, 2, cc, rho:rho + 1], ALU.max),
                                 r=[dn_, cols], w=[dn_])
                            P.op("vector", lambda e, dn_=dn_: e.reciprocal(dn_[:, 0:1], dn_[:, 0:1]), r=[dn_], w=[dn_])
                            P.op("vector", lambda e, ho_=ho_, num_=num_, dn_=dn_: e.tensor_scalar(ho_[:], num_[:, 0:DH], dn_[:, 0:1], None, ALU.mult),
                                 r=[num_, dn_], w=[ho_])
                            row0 = s * NT + cc * 128
                            P.dma(self.HS[d][row0:row0 + 128, h * DH:(h + 1) * DH], ho_[:], r=[ho_])
                            P.op("gpsimd", lambda e, kw_=kw_, cc=cc, cols=cols, rho=rho: e.tensor_scalar(
                                kw_[:], ks[:, cc, :], cols[:, 3, cc, rho:rho + 1], 1.0 / 16, ALU.mult, ALU.mult), r=[ks, cols], w=[kw_])
                            for dc in range(DC):
                                pu = pU[dc % 2]
                                P.mm(pu[:, 0:DH + 1], kw_[:, dc * 128:(dc + 1) * 128], va[:, cc, 0:DH + 1], True, True, r=[kw_, va], w=[pu])
                                P.op("vector", lambda e, pu=pu, dc=dc, cc=cc: e.scalar_tensor_tensor(
                                    CA[:, dc, 0:DH + 1], CA[:, dc, 0:DH + 1], wpb[:, cc:cc + 1], pu[:, 0:DH + 1], ALU.mult, ALU.add),
                                    r=[CA, wpb, pu], w=[CA])

    def s_mlout(self, l):
        c, P, I = self.cfg, self.P, self.I
        MW, MH, DH = c.MW, c.MH, c.DH
        ntiles = c.B * c.TT
        with P.stage():
            gain = P.sb("gain", [128, MW])
            P.dma(gain[:], I["ml_gain"][l:l + 1, :].broadcast_to([128, MW]), w=[gain])
            hf = [P.sb("hf%d" % i, [128, MW]) for i in range(2)]
            hb = [P.sb("hb%d" % i, [128, MW]) for i in range(2)]
            po = [P.sb("po%d" % i, [128, MW]) for i in range(2)]
            sq = [P.sb("sq%d" % i, [128, MW]) for i in range(2)]
            ss = [P.sb("ss%d" % i, [128, 4 * ((MH + 3) // 4)]) for i in range(2)]
            oo = c.off["ml_o"]
            for gt in range(ntiles):
                i2 = gt % 2
                hf_, hb_, po_, sq_, ss_ = hf[i2], hb[i2], po[i2], sq[i2], ss[i2]
                rows = slice(gt * 128, (gt + 1) * 128)
                P.dma(hf_[:], self.HS[0][rows, :], w=[hf_])
                P.dma(hb_[:], self.HS[1][rows, :], w=[hb_])
                P.dma(po_[:], self.PH[rows, oo:oo + MW], w=[po_])
                P.op("vector", lambda e, hf_=hf_, hb_=hb_: e.tensor_tensor(hf_[:], hf_[:], hb_[:], ALU.add), r=[hf_, hb_], w=[hf_])
                P.op("gpsimd", lambda e, hf_=hf_, sq_=sq_: e.tensor_tensor(sq_[:], hf_[:], hf_[:], ALU.mult), r=[hf_], w=[sq_])
                P.op("vector", lambda e, sq_=sq_, ss_=ss_: e.reduce_sum(ss_[:, 0:MH], sq_[:].rearrange("p (h d) -> p h d", d=DH), AX.X), r=[sq_], w=[ss_])
                P.op("vector", lambda e, ss_=ss_: e.tensor_scalar(ss_[:, 0:MH], ss_[:, 0:MH], 1.0 / DH, c.EPS, ALU.mult, ALU.add), r=[ss_], w=[ss_])
                P.act(ss_[:, 0:MH], ss_[:, 0:MH], AF.Sqrt, r=[ss_], w=[ss_])
                P.op("vector", lambda e, ss_=ss_: e.reciprocal(ss_[:, 0:MH], ss_[:, 0:MH]), r=[ss_], w=[ss_])
                P.op("vector", lambda e, hf_=hf_, ss_=ss_: e.tensor_tensor(
                    hf_[:].rearrange("p (h d) -> p h d", d=DH), hf_[:].rearrange("p (h d) -> p h d", d=DH),
                    ss_[:, 0:MH].rearrange("p (h o) -> p h o", o=1).broadcast_to([128, MH, DH]), ALU.mult), r=[hf_, ss_], w=[hf_])
                P.op("gpsimd", lambda e, hf_=hf_: e.tensor_tensor(hf_[:], hf_[:], gain[:], ALU.mult), r=[hf_, gain], w=[hf_])
                P.act(po_[:], po_[:], AF.Sigmoid, r=[po_], w=[po_])
                P.op("vector", lambda e, hf_=hf_, po_=po_: e.tensor_tensor(hf_[:], hf_[:], po_[:], ALU.mult), r=[hf_, po_], w=[hf_])
                P.dma(self.MIX[rows, c.AW + c.GW:c.AW + c.GW + MW], hf_[:], r=[hf_])

    def live_tiles(self, l):
        c = self.cfg
        if l == c.DEPTH - 1:
            return [s * c.TT + t for s in range(c.B) for t in range(c.CT, c.TT)]
        return list(range(c.B * c.TT))

    def s_outproj(self, l):
        c, P, I = self.cfg, self.P, self.I
        KM = (c.AW + c.GW + c.MW) // 128
        G = 8
        tiles = self.live_tiles(l)
        if not hasattr(self, "MO"):
            self.MO = self.hbm_split("MO", c.D, dump=True)
        with P.stage():
            mT = P.sb("mT", [128, KM, G * 128], BF16)
            xs = [P.sb("x%d" % i, [128, KM * 128]) for i in range(2)]
            pst = [P.ps("pst%d" % i, [128, 4, 128]) for i in range(2)]
            NCH = 512
            wsl = [P.sb("wsl%d" % i, [128, KM, NCH], BF16) for i in range(2)]
            psm = [P.ps("psm%d" % i, [128, NCH]) for i in range(2)]
            osb = [P.sb("osb%d" % i, [128, NCH]) for i in range(3)]
            it = 0; wi = 0; pi = 0
            for g0 in range(0, len(tiles), G):
                gts = tiles[g0:g0 + G]
                for gi, gt in enumerate(gts):
                    x_ = xs[gi % 2]
                    P.dma(x_[:], self.MIX[gt * 128:(gt + 1) * 128, :], w=[x_])
                    for k0 in range(0, KM, 4):
                        ps = pst[pi % 2]; pi += 1
                        n = min(4, KM - k0)
                        for k in range(k0, k0 + n):
                            P.op("tensor", lambda e, ps=ps, k=k, k0=k0, x_=x_: e.transpose(ps[:, k - k0, :], x_[:, k * 128:(k + 1) * 128], self.ident[:]),
                                 r=[x_, self.ident], w=[ps])
                        if (k0 // 4) % 2:
                            P.op("vector", lambda e, ps=ps, k0=k0, n=n, gi=gi: e.tensor_copy(mT[:, k0:k0 + n, gi * 128:(gi + 1) * 128], ps[:, 0:n, :]),
                                 r=[ps], w=[(mT, (k0, gi))])
                        else:
                            P.act(mT[:, k0:k0 + n, gi * 128:(gi + 1) * 128], ps[:, 0:n, :], AF.Copy, r=[ps], w=[(mT, (k0, gi))])
                for n0 in range(0, c.D, NCH):
                    nw = min(NCH, c.D - n0)
                    w_ = wsl[wi % 2]; wi += 1
                    P.dma_split(w_[:, :, :nw], I["w_out"][l, :, n0:n0 + nw].rearrange("(kc p) n -> p kc n", p=128), w=[w_], queue="gpsimd")
                    for gi, gt in enumerate(gts):
                        pm = psm[it % 2]; o_ = osb[it % 3]; it += 1
                        for k in range(KM):
                            P.mm(pm[:, :nw], mT[:, k, gi * 128:(gi + 1) * 128], w_[:, k, :nw], k == 0, k == KM - 1, r=[mT, w_], w=[pm])
                        if it % 2:
                            P.op("vector", lambda e, o_=o_, pm=pm, nw=nw: e.tensor_copy(o_[:, :nw], pm[:, :nw]), r=[pm], w=[o_])
                        else:
                            P.act(o_[:, :nw], pm[:, :nw], AF.Copy, r=[pm], w=[o_])
                        P.dma(self.MO[gt * 128:(gt + 1) * 128, n0:n0 + nw], o_[:, :nw], r=[o_])

    def s_resln(self, l, which):
        c, P, I = self.cfg, self.P, self.I
        KC, D, NE = c.KC, c.D, c.NE
        tiles = self.live_tiles(l)
        last = l == c.DEPTH - 1
        if not hasattr(self, "BTOK"):
            self.BTOK = [self.hbm("BTOK_%d" % s, [c.NT, D], BF16) for s in range(c.B)]
            self.AFF = self.hbm_split("AFF", NE, dump=True)
            self.MOE = self.hbm_split("MOE", D, dump=True)
            self.OUT = P.out("out", [c.B, c.SEQ, D])
        if which == 1:
            SRC, gj, lg, lb, DST = self.MO, 2, I["ln1_g"], I["ln1_b"], self.H1
        else:
            SRC, gj, lg, lb, DST = self.MOE, 5, I["ln2_g"], I["ln2_b"], self.H2
        mT = self.modT[l]
        with P.stage():
            lng = P.sb("lng", [128, D]); lnb = P.sb("lnb", [128, D])
            P.dma(lng[:], lg[l:l + 1, :].broadcast_to([128, D]), w=[lng])
            P.dma(lnb[:], lb[l:l + 1, :].broadcast_to([128, D]), w=[lnb])
            gb = [P.sb("gb%d" % i, [128, D]) for i in range(2)]
            P.dma(gb[1][:], self.MODR[l, c.B:c.B + 1, gj * D:(gj + 1) * D].broadcast_to([128, D]), w=[gb[1]])
            mo = [P.sb("mo%d" % i, [128, D]) for i in range(2)]
            hh = [P.sb("hh%d" % i, [128, D]) for i in range(2)]
            stat = [P.sb("stat%d" % i, [128, 8, 6]) for i in range(2)]
            mv = [P.sb("mv%d" % i, [128, 2]) for i in range(2)]
            rstd = [P.sb("rstd%d" % i, [128, 1]) for i in range(2)]
            if which == 1:
                wr = P.sb("wr", [128, KC, NE])
                P.dma_split(wr[:], I["w_router"][l].rearrange("(kc p) n -> p kc n", p=128), w=[wr])
                bT32 = [P.sb("bT32%d" % i, [128, KC, 128]) for i in range(1)] * 2
                bT16 = [P.sb("bT16%d" % i, [128, KC, 128], BF16) for i in range(1)] * 2
                btok = [P.sb("btok%d" % i, [128, D], BF16) for i in range(2)]
                psb = [P.ps("psb%d" % i, [128, 8, 128], BF16) for i in range(2)]
                pst = [P.ps("pst%d" % i, [128, 4, 128]) for i in range(2)]
                plg = [P.ps("plg%d" % i, [128, 512]) for i in range(2)]
                sm = [P.sb("sm%d" % i, [128, 8]) for i in range(2)]
                af = [P.sb("af%d" % i, [128, NE]) for i in range(2)]
            cur_s = -1
            pi = 0
            for ti, gt in enumerate(tiles):
                s, t = gt // c.TT, gt % c.TT
                i2 = ti % 2
                mo_, h_, st_, mv_, rs_ = mo[i2], hh[i2], stat[i2], mv[i2], rstd[i2]
                if s != cur_s:
                    cur_s = s
                    P.dma(gb[0][:], self.MODR[l, s:s + 1, gj * D:(gj + 1) * D].broadcast_to([128, D]), w=[gb[0]])
                g_ = gb[1] if t < c.CT else gb[0]
                rows = slice(gt * 128, (gt + 1) * 128)
                P.dma(mo_[:], SRC[rows, :], w=[mo_])
                P.dma(h_[:], self.h_src(l, s, t) if which == 1 else self.H1[rows, :], w=[h_])
                P.op("gpsimd", lambda e, mo_=mo_, g_=g_: e.tensor_tensor(mo_[:], mo_[:], g_[:], ALU.mult), r=[mo_, g_], w=[mo_])
                P.op("vector", lambda e, mo_=mo_, h_=h_: e.scalar_tensor_tensor(mo_[:], h_[:], c.ALPHA, mo_[:], ALU.mult, ALU.add), r=[mo_, h_], w=[mo_])
                self.ln_stats(mo_, st_, mv_, rs_, D)
                P.op("vector", lambda e, mo_=mo_, h_=h_, mv_=mv_, rs_=rs_: e.tensor_scalar(h_[:], mo_[:], mv_[:, 0:1], rs_[:, 0:1], ALU.subtract, ALU.mult),
                     r=[mo_, mv_, rs_], w=[h_])
                P.op("gpsimd", lambda e, h_=h_: e.tensor_tensor(h_[:], h_[:], lng[:], ALU.mult), r=[h_, lng], w=[h_])
                P.op("vector", lambda e, h_=h_: e.tensor_tensor(h_[:], h_[:], lnb[:], ALU.add), r=[h_, lnb], w=[h_])
                if which == 2 and last:
                    P.dma(self.OUT[s, (t - c.CT) * 128:(t - c.CT + 1) * 128, :], h_[:], r=[h_], w=["OUT"])
                    continue
                P.dma(DST[rows, :], h_[:], r=[h_])
                if which == 2:
                    continue
                mrow = c.B if t < c.CT else s
                self.ln_stats(h_, st_, mv_, rs_, D)
                P.op("vector", lambda e, mo_=mo_, h_=h_, mv_=mv_, rs_=rs_: e.tensor_scalar(mo_[:], h_[:], mv_[:, 0:1], rs_[:, 0:1], ALU.subtract, ALU.mult),
                     r=[h_, mv_, rs_], w=[mo_])
                b32, b16 = bT32[i2], bT16[i2]
                for k0 in range(0, KC, 4):
                    ps = pst[pi % 2]; pi += 1
                    n = min(4, KC - k0)
                    for k in range(k0, k0 + n):
                        P.op("tensor", lambda e, ps=ps, k=k, k0=k0, mo_=mo_: e.transpose(ps[:, k - k0, :], mo_[:, k * 128:(k + 1) * 128], self.ident[:]),
                             r=[mo_, self.ident], w=[ps])
                    for k in range(k0, k0 + n):
                        P.act(b32[:, k, :], ps[:, k - k0, :], AF.Identity, r=[ps, mT], w=[(b32, k)],
                              scale=mT[:, 4, k, mrow:mrow + 1], bias=mT[:, 3, k, mrow:mrow + 1])
                P.op("vector", lambda e, b32=b32, b16=b16: e.tensor_copy(b16[:], b32[:]), r=[b32], w=[b16])
                bk = btok[i2]
                for k0 in range(0, KC, 8):
                    pb = psb[(k0 // 8) % 2]
                    n = min(8, KC - k0)
                    for k in range(k0, k0 + n):
                        P.op("tensor", lambda e, pb=pb, k=k, k0=k0, b16=b16: e.transpose(pb[:, k - k0, :], b16[:, k, :], self.identb[:]),
                             r=[b16, self.identb], w=[pb])
                    P.op("gpsimd" if False else "vector", lambda e, pb=pb, k0=k0, n=n, bk=bk: e.tensor_copy(
                        bk[:, k0 * 128:(k0 + n) * 128].rearrange("p (k d) -> p k d", d=128), pb[:, 0:n, :]), r=[pb], w=[(bk, k0)])
                P.dma(self.BTOK[s][t * 128:(t + 1) * 128, :], bk[:], r=[bk])
                pl = plg[i2]
                for k in range(KC):
                    P.mm(pl[:, 0:NE], b32[:, k, :], wr[:, k, :], k == 0, k == KC - 1, r=[b32, wr], w=[pl])
                sm_, af_ = sm[i2], af[i2]
                P.op("vector", lambda e, sm_=sm_, pl=pl: e.reduce_max(sm_[:, 0:1], pl[:, 0:NE], AX.X), r=[pl], w=[sm_])
                P.op("vector", lambda e, sm_=sm_: e.tensor_scalar(sm_[:, 1:2], sm_[:, 0:1], -1.0, None, ALU.mult), r=[sm_], w=[sm_])
                P.act(af_[:], pl[:, 0:NE], AF.Exp, r=[pl, sm_], w=[af_, (sm_, "s")], bias=sm_[:, 1:2], accum_out=sm_[:, 2:3])
                P.op("vector", lambda e, sm_=sm_: e.reciprocal(sm_[:, 3:4], sm_[:, 2:3]), r=[sm_, (sm_, "s")], w=[sm_])
                P.op("vector", lambda e, sm_=sm_, af_=af_: e.tensor_scalar(af_[:], af_[:], sm_[:, 3:4], None, ALU.mult), r=[af_, sm_], w=[af_])
                P.dma(self.AFF[rows, :], af_[:], r=[af_])

    def s_moe(self, l):
        c, P, I = self.cfg, self.P, self.I
        NE, NT, TT, CT, D, KC, FF = c.NE, c.NT, c.TT, c.CT, c.D, c.KC, c.FF
        last = l == c.DEPTH - 1
        R = c.B * NE
        cap_l = 2 * c.SEQ // NE
        cap_c = 2 * c.CTX // NE
        LTl = cap_l // 128
        assert cap_l % 128 == 0 and cap_c <= 128 and cap_c % 8 == 0
        nst = LTl + 1
        PIECE = min(D, getattr(c, "PIECE", 512))
        NPC = D // PIECE
        if not hasattr(self, "IDXC"):
            self.IDXC = P.sb("idxc", [128, nst, R], mybir.dt.uint32)
            self.IDX8F = P.sb("idx8f", [128, nst, R])
            self.GATC = P.sb("gatc", [128, nst, R])
            self.TOPD = self.hbm("TOPD", [R, cap_l + cap_c], dump=True)
        U32 = mybir.dt.uint32
        with P.stage():
            A = P.sb("A", [R, NT])
            stg = P.sb("stg", [NE, NT])
            af = [P.sb("af%d" % i, [128, NE]) for i in range(2)]
            pst = [P.ps("pst%d" % i, [128, 4, 128]) for i in range(2)]
            pi = 0
            t_lo = CT if last else 0
            for s in range(c.B):
                for t in range(t_lo, TT):
                    gt = s * TT + t
                    a_ = af[gt % 2]
                    P.dma(a_[:], self.AFF[gt * 128:(gt + 1) * 128, :], w=[a_])
                    if (t - t_lo) % 4 == 0:
                        ps = pst[pi % 2]; pi += 1
                        t0 = t
                    P.mm(ps[:NE, t - t0, :], a_[:, :], self.ident[:], True, True, r=[a_, self.ident], w=[ps])
                    if t - t0 == 3 or t == TT - 1:
                        n = t - t0 + 1
                        P.op("vector", lambda e, ps=ps, t0=t0, n=n: e.tensor_copy(
                            stg[:, t0 * 128:(t0 + n) * 128].rearrange("r (t p) -> r t p", p=128), ps[:NE, 0:n, :]), r=[ps], w=[stg])
                P.dma(A[s * NE:(s + 1) * NE, t_lo * 128:NT], stg[:, t_lo * 128:NT], r=[stg], w=[A])
            TV = P.sb("TV", [R, cap_l + cap_c]); TI = P.sb("TI", [R, cap_l + cap_c], U32); TF = P.sb("TF", [R, cap_l + cap_c])
            sets = [(c.CTX, NT, cap_l, 0, float(c.CTX))]
            if not last:
                sets.append((0, c.CTX, cap_c, cap_l, 0.0))
            for (lo, hi, cap, o0, addv) in sets:
                for r_ in range(cap // 8):
                    sl = slice(o0 + r_ * 8, o0 + r_ * 8 + 8)
                    P.op("vector", lambda e, sl=sl, lo=lo, hi=hi: e.max(TV[:, sl], A[:, lo:hi]), r=[A], w=[(TV, r_ + o0)])
                    P.op("vector", lambda e, sl=sl, lo=lo, hi=hi: e.max_index(TI[:, sl], TV[:, sl], A[:, lo:hi]), r=[A, (TV, r_ + o0)], w=[(TI, r_ + o0)])
                    P.op("vector", lambda e, sl=sl, lo=lo, hi=hi: e.match_replace(A[:, lo:hi], TV[:, sl], A[:, lo:hi], -1.0), r=[(TV, r_ + o0), (TI, r_ + o0)], w=[A])
                P.op("vector", lambda e, o0=o0, cap=cap: e.tensor_copy(TF[:, o0:o0 + cap], TI[:, o0:o0 + cap]), r=[TI], w=[(TF, o0)])
                if addv:
                    P.op("vector", lambda e, o0=o0, cap=cap, addv=addv: e.tensor_scalar(TF[:, o0:o0 + cap], TF[:, o0:o0 + cap], addv, None, ALU.add),
                         r=[(TF, o0)], w=[(TF, o0)])
            P.dma(self.TOPD, TF[:], r=[TF])
            for (srcT, dstC, isidx) in ((TF, self.IDXC, True), (TV, self.GATC, False)):
                blocks = [(j, j * 128, 128) for j in range(LTl)]
                if not last:
                    blocks.append((LTl, cap_l, cap_c))
                for (j, c0, n) in blocks:
                    ps = pst[pi % 2]; pi += 1
                    P.op("tensor", lambda e, ps=ps, c0=c0, n=n, srcT=srcT: e.transpose(ps[:n, 0, 0:R], srcT[:R, c0:c0 + n], self.ident[:R, :R]),
                         r=[srcT, self.ident], w=[ps])
                    P.op("vector", lambda e, ps=ps, j=j, n=n, dstC=dstC: e.tensor_copy(dstC[:n, j, :], ps[:n, 0, 0:R]), r=[ps], w=[dstC])
                    if isidx:
                        P.op("vector", lambda e, ps=ps, j=j, n=n: e.tensor_scalar(
                            self.IDX8F[:n, j, :], ps[:n, 0, 0:R], float(NPC), None, ALU.mult), r=[ps], w=[self.IDX8F])
        MS = cap_l + (0 if last else cap_c)
        chunks = [(0, cap_l)] if cap_l <= 512 else [(i, min(512, cap_l - i)) for i in range(0, cap_l, 512)]
        if not last:
            chunks.append((cap_l, cap_c))
        tiles_ = [(j, j * 128, 128) for j in range(LTl)] + ([] if last else [(LTl, cap_l, cap_c)])
        FJ = FF // 128
        with P.stage():
            ysb = [P.sb("ysb%d" % i, [128, D]) for i in range(len(tiles_))]
            zt = ysb[0]
            P.op("vector", lambda e: e.memset(zt[:], 0.0), w=[zt])
            for s in range(c.B):
                for t in range(TT):
                    P.dma(self.MOE[slice((s * TT + t) * 128, (s * TT + t + 1) * 128), slice(None)], zt[:], r=[zt], w=["MOE"])
            xg = [P.sb("xg%d" % i, [128, D], BF16) for i in range(1)] * 2
            xeT = P.sb("xeT", [128, KC, MS], BF16)
            hid = P.sb("hid", [128, FJ, MS], BF16)
            wg = [P.sb("wg%d" % i, [128, KC, 128], BF16) for i in range(2)]
            wu = [P.sb("wu%d" % i, [128, KC, 128], BF16) for i in range(1)] * 2
            wd = [P.sb("wd%d" % i, [128, FJ, 512], BF16) for i in range(1)] * 2
            sg = [P.sb("sg%d" % i, [128, 512]) for i in range(1)] * 2
            psb = [P.ps("psb%d" % i, [128, 8, 128], BF16) for i in range(2)]
            pg = [P.ps("pg%d" % i, [128, 512]) for i in range(2)]
            pu = [P.ps("pu%d" % i, [128, 512]) for i in range(2)]
            py = [P.ps("py%d" % i, [128, 512]) for i in range(2)]
            pcol = P.sb("pcol", [128, max(NPC, 4)])
            for p_ in range(NPC):
                P.op("vector", lambda e, p_=p_: e.memset(pcol[:, p_:p_ + 1], float(p_)), w=[(pcol, p_)])
            idxp = [P.sb("idxp%d" % i, [128, max(NPC, 4)], mybir.dt.uint32) for i in range(2)]
            ipi = 0
            gi = 0; wi = 0; di = 0; yi = 0; pbi = 0; ci = 0
            import os
            MLIM = int(os.environ.get("MLIM", "9"))
            for s in range(c.B):
                for e_ in range(NE):
                    rr = s * NE + e_
                    if MLIM == 0:
                        continue
                    for (j, c0, n) in tiles_:
                        x_ = xg[gi % 2]; gi += 1
                        P.idma(lambda e, x_=x_, j=j, n=n, s=s, rr=rr: e.indirect_dma_start(
                            out=x_[:n, :], out_offset=None, in_=self.BTOK[s][:, :],
                            in_offset=bass.IndirectOffsetOnAxis(self.IDXC[:n, j, rr:rr + 1], 0)), r=[self.IDXC], w=[x_])
                        for k0 in range(0, KC, 8):
                            pb = psb[pbi % 2]; pbi += 1
                            nk = min(8, KC - k0)
                            for k in range(k0, k0 + nk):
                                P.op("tensor", lambda e, pb=pb, k=k, k0=k0, x_=x_, n=n: e.transpose(
                                    pb[:, k - k0, 0:n], x_[:n, k * 128:(k + 1) * 128], self.identb[:n, :n]), r=[x_, self.identb], w=[pb])
                            if (k0 // 8) % 2:
                                P.op("vector", lambda e, pb=pb, k0=k0, nk=nk, c0=c0, n=n: e.tensor_copy(xeT[:, k0:k0 + nk, c0:c0 + n], pb[:, 0:nk, 0:n]),
                                     r=[pb], w=[(xeT, (k0, j))])
                            else:
                                P.act(xeT[:, k0:k0 + nk, c0:c0 + n], pb[:, 0:nk, 0:n], AF.Copy, r=[pb], w=[(xeT, (k0, j))])
                    if MLIM == 1:
                        continue
                    for fj in range(FJ):
                        wg_, wu_ = wg[wi % 2], wu[wi % 2]; wi += 1
                        P.dma_split(wg_[:], I["w_e_gate"][l, e_, :, fj * 128:(fj + 1) * 128].rearrange("(kc p) f -> p kc f", p=128), w=[wg_], queue="gpsimd")
                        P.dma_split(wu_[:], I["w_e_up"][l, e_, :, fj * 128:(fj + 1) * 128].rearrange("(kc p) f -> p kc f", p=128), w=[wu_], queue="gpsimd")
                        for (c0, n) in chunks:
                            pg_, pu_, sg_ = pg[ci % 2], pu[ci % 2], sg[ci % 2]; ci += 1
                            for k in range(KC):
                                P.mm(pg_[:, 0:n], wg_[:, k, :], xeT[:, k, c0:c0 + n], k == 0, k == KC - 1, r=[wg_, xeT], w=[pg_])
                            for k in range(KC):
                                P.mm(pu_[:, 0:n], wu_[:, k, :], xeT[:, k, c0:c0 + n], k == 0, k == KC - 1, r=[wu_, xeT], w=[pu_])
                            P.act(sg_[:, 0:n], pg_[:, 0:n], AF.Silu, r=[pg_], w=[sg_])
                            P.op("vector", lambda e, sg_=sg_, pu_=pu_, fj=fj, c0=c0, n=n: e.tensor_tensor(hid[:, fj, c0:c0 + n], sg_[:, 0:n], pu_[:, 0:n], ALU.mult),
                                 r=[sg_, pu_], w=[(hid, (fj, c0))])
                    y_tiles = {}
                    for d0 in range(0, D, 512):
                        wd_ = wd[di % 2]; di += 1
                        P.dma(wd_[:], I["w_e_down"][l, e_, :, d0:d0 + 512].rearrange("(fj p) d -> p fj d", p=128), w=[wd_], queue="gpsimd")
                        for (j, c0, n) in tiles_:
                            y_tiles[j] = ysb[j]
                            py_ = py[(j + d0 // 512) % 2]
                            for fj in range(FJ):
                                P.mm(py_[:n, :], hid[:, fj, c0:c0 + n], wd_[:, fj, :], fj == 0, fj == FJ - 1, r=[hid, wd_], w=[py_])
                            y_ = y_tiles[j]
                            P.op("vector", lambda e, y_=y_, py_=py_, d0=d0, n=n, j=j, rr=rr: e.tensor_scalar(
                                y_[:n, d0:d0 + 512], py_[:n, :], self.GATC[:n, j, rr:rr + 1], None, ALU.mult), r=[py_, self.GATC], w=[(y_, d0)])
                    if MLIM == 3:
                        continue
                    for (j, c0, n) in tiles_:
                        y_ = y_tiles[j]
                        ip = idxp[ipi % 2]; ipi += 1
                        P.op("vector", lambda e, ip=ip, j=j, n=n, rr=rr: e.tensor_scalar(
                            ip[:n, 0:NPC], pcol[:n, 0:NPC], self.IDX8F[:n, j, rr:rr + 1], None, ALU.add), r=[pcol, self.IDX8F], w=[ip])
                        for p_ in range(NPC):
                            P.idma(lambda e, y_=y_, ip=ip, n=n, s=s, p_=p_: e.indirect_dma_start(
                                out=self.MOE.aps[s].rearrange("n (p e) -> (n p) e", e=PIECE),
                                out_offset=bass.IndirectOffsetOnAxis(ip[:n, p_:p_ + 1], 0),
                                in_=y_[:n, p_ * PIECE:(p_ + 1) * PIECE], in_offset=None, compute_op=ALU.add),
                                r=[y_, ip], w=["MOE"], key=y_.name)

    def build_all(self):
        c = self.cfg
        self.declare()
        for l in range(c.DEPTH):
            self.s_mod(l)
        for l in range(c.DEPTH):
            self.s_proj(l)
            self.s_qk(l)
            self.s_attn(l)
            self.s_gmlp(l)
            self.s_gates(l)
            self.s_mlstm(l)
            self.s_mlout(l)
            self.s_outproj(l)
            self.s_resln(l, 1)
            self.s_moe(l)
            self.s_resln(l, 2)
        return self


def run_model(cfg, inputs, debug=False):
    M = Model(cfg, debug=debug).build_all()
    feed = {k: np.ascontiguousarray(np.asarray(v, dtype=np.float32)) for k, v in inputs.items()}
    feed["c_ctx"] = feed["c_ctx"].reshape(1, -1)
    for k, v in host_consts(cfg).items():
        feed["k_" + k] = v
    nc = M.P.build()
    res = run_bass_kernel_spmd(nc, [feed], core_ids=[0])
    return res.results[0]


def kernel(**inputs):
    cfg = Cfg()
    res = run_model(cfg, inputs)
    return np.asarray(res["out"], dtype=np.float32)
```
